# Optimizing a Trainium2 kernel written in Bass

```python
import math
import functools
import jax
import jax.numpy as jnp
from jax import lax
import numpy as np

D_MODEL = 1024
BATCH = 4
SEQ = 8192
DEPTH = 4

GRID_W = 64
CTX_LEN = 256
N_MIXERS = 3
NORM_EPS = 1e-6
MOD_CHUNKS = 6
SHORT_CONV_WIDTH = 4
SHORT_CONV_PAD = (2, 1)
LRU_WIDTH = 1280
LRU_BLOCKS = 10
LRU_BLOCK = LRU_WIDTH // LRU_BLOCKS
RG_C = 8.0
HGRN_HEAD_DIM = 128
HGRN_HEADS = D_MODEL // HGRN_HEAD_DIM
HGRN_CHUNK = 32
SSD_INNER = 2 * D_MODEL
SSD_HEAD_DIM = 64
SSD_HEADS = SSD_INNER // SSD_HEAD_DIM
SSD_GROUPS = 8
SSD_HPG = SSD_HEADS // SSD_GROUPS
SSD_STATE = 128
SSD_BC = SSD_GROUPS * SSD_STATE
SSD_CONV_DIM = SSD_INNER + 2 * SSD_BC
SSD_CHUNK = 64
FFN_HIDDEN = 2816
FFN_CONV = 3

kernel_name = 'hybrid_rglru_hgrn2_ssd_convffn_dit'


def rms_norm(x, eps=NORM_EPS):
    xf = x.astype(jnp.float32)
    return (xf * lax.rsqrt(jnp.mean(xf * xf, axis=-1, keepdims=True) + eps)).astype(x.dtype)


def modulate(h, shift, scale):
    return h * (1.0 + scale) + shift


def dwconv1d(x, w, b, pad):
    y = lax.conv_general_dilated(x, w[:, None, :].astype(x.dtype), window_strides=(1,), padding=[pad],
                                 dimension_numbers=('NWC', 'WIO', 'NWC'), feature_group_count=x.shape[-1])
    return y + b.astype(x.dtype)


def dwconv2d(x, w, b):
    y = lax.conv_general_dilated(x, w[:, :, None, :].astype(x.dtype), window_strides=(1, 1),
                                 padding=[(1, 1), (1, 1)], dimension_numbers=('NHWC', 'HWIO', 'NHWC'),
                                 feature_group_count=x.shape[-1])
    return y + b.astype(x.dtype)


def flip_time(tree):
    return jax.tree_util.tree_map(lambda t: jnp.flip(t, axis=1), tree)


def run_bidirectional(scan_fn, ctx_fwd, lat_fwd, ctx_bwd, lat_bwd, state0):
    oc_f, s_f = scan_fn(ctx_fwd, state0)
    ol_f, _ = scan_fn(lat_fwd, s_f)
    oc_b, s_b = scan_fn(flip_time(ctx_bwd), state0)
    ol_b, _ = scan_fn(flip_time(lat_bwd), s_b)
    return oc_f + flip_time(oc_b), ol_f + flip_time(ol_b)


def chunked_scan(step, chunk, inputs, state0):
    bsz, length = inputs[0].shape[:2]
    n = length // chunk
    xs = tuple(jnp.moveaxis(t.reshape(bsz, n, chunk, *t.shape[2:]), 1, 0) for t in inputs)
    state, ys = lax.scan(step, state0, xs)
    return jnp.moveaxis(ys, 0, 1).reshape(bsz, length, *ys.shape[3:]), state


def lru_combine(left, right):
    a_l, b_l = left
    a_r, b_r = right
    return a_l * a_r, a_r * b_l + b_r


def lru_scan(inputs, h0):
    a, b = inputs
    b = b.at[:, 0].add(a[:, 0] * h0)
    _, h = lax.associative_scan(lru_combine, (a, b), axis=1)
    return h, h[:, -1]


def lru_gates(xc, gate_w, gate_b, lam):
    bsz, length, _ = xc.shape
    xh = xc.reshape(bsz, length, LRU_BLOCKS, LRU_BLOCK)
    pre = jnp.einsum('blhi,ghij->gblhj', xh, gate_w.astype(jnp.float32)) + gate_b.astype(jnp.float32)[:, None, None]
    r, i = jax.nn.sigmoid(pre).reshape(2, bsz, length, LRU_WIDTH)
    log_a = -RG_C * jax.nn.softplus(-lam.astype(jnp.float32)) * r
    return jnp.exp(log_a), jnp.sqrt(-jnp.expm1(2.0 * log_a)) * (i * xc)


def rglru_mixer(h_ctx, h_lat, w_in, conv_w, conv_b, gate_w, gate_b, lam, w_out):
    def prep(h):
        y_branch, x_branch = jnp.split(h @ w_in, 2, axis=-1)
        xc = dwconv1d(x_branch, conv_w, conv_b, SHORT_CONV_PAD).astype(jnp.float32)
        return y_branch, lru_gates(xc, gate_w[0], gate_b[0], lam[0]), lru_gates(xc, gate_w[1], gate_b[1], lam[1])
    y_ctx, ctx_f, ctx_b = prep(h_ctx)
    y_lat, lat_f, lat_b = prep(h_lat)
    h0 = jnp.zeros((h_ctx.shape[0], LRU_WIDTH), jnp.float32)
    r_ctx, r_lat = run_bidirectional(lru_scan, ctx_f, lat_f, ctx_b, lat_b, h0)
    out = lambda y, r: (jax.nn.gelu(y) * r.astype(y.dtype)) @ w_out
    return out(y_ctx, r_ctx), out(y_lat, r_lat)


def gla_chunk_step(state, chunk):
    q, k, v, logf = chunk
    size = q.shape[1]
    gcum = jnp.cumsum(logf, axis=1)
    q_dec = q * jnp.exp(gcum)
    k_inv = k * jnp.exp(-gcum)
    k_end = k * jnp.exp(gcum[:, -1:] - gcum)
    tri = jnp.tril(jnp.ones((size, size), dtype=bool))
    scores = jnp.where(tri, jnp.einsum('bthk,bshk->bhts', q_dec, k_inv), 0.0)
    out = jnp.einsum('bhts,bshv->bthv', scores, v) + jnp.einsum('bthk,bhkv->bthv', q_dec, state)
    state = state * jnp.exp(gcum[:, -1])[..., None] + jnp.einsum('bshk,bshv->bhkv', k_end, v)
    return state, out


def hgrn2_mixer(h_ctx, h_lat, lb, w_in, norm_w, w_out):
    def prep(h):
        bsz, length, _ = h.shape
        heads = lambda t: t.astype(jnp.float32).reshape(bsz, length, HGRN_HEADS, HGRN_HEAD_DIM)
        q, v, f_fwd, f_bwd, g = jnp.split(h @ w_in, 5, axis=-1)
        q, v = heads(jax.nn.silu(q)), heads(v)
        def direction(f, lower):
            forget = lower + (1.0 - lower) * jax.nn.sigmoid(f.astype(jnp.float32))
            return q, heads(1.0 - forget), v, heads(jnp.log(forget))
        return g, direction(f_fwd, lb[0]), direction(f_bwd, lb[1])
    g_ctx, ctx_f, ctx_b = prep(h_ctx)
    g_lat, lat_f, lat_b = prep(h_lat)
    s0 = jnp.zeros((h_ctx.shape[0], HGRN_HEADS, HGRN_HEAD_DIM, HGRN_HEAD_DIM), jnp.float32)
    scan = functools.partial(chunked_scan, gla_chunk_step, HGRN_CHUNK)
    o_ctx, o_lat = run_bidirectional(scan, ctx_f, lat_f, ctx_b, lat_b, s0)
    def out(o, g):
        o = (rms_norm(o) * norm_w.astype(jnp.float32)).reshape(g.shape).astype(g.dtype)
        return (o * jax.nn.silu(g)) @ w_out
    return out(o_ctx, g_ctx), out(o_lat, g_lat)


def ssd_chunk_step(state, chunk):
    x, dt, da, bm, cm = chunk
    bsz, size = x.shape[:2]
    grp = lambda t: t.reshape(*t.shape[:-1], SSD_GROUPS, SSD_HPG)
    acum = jnp.cumsum(da, axis=1)
    tri = jnp.tril(jnp.ones((size, size), dtype=bool))
    seg = acum[:, :, None, :] - acum[:, None, :, :]
    decay = jnp.exp(jnp.where(tri[None, :, :, None], seg, -jnp.inf))
    scores = jnp.einsum('btgn,bsgn->btsg', cm, bm)[..., None] * grp(decay)
    xdt = (x * dt[..., None]).reshape(bsz, size, SSD_GROUPS, SSD_HPG, SSD_HEAD_DIM)
    s_g = state.reshape(bsz, SSD_GROUPS, SSD_HPG, SSD_HEAD_DIM, SSD_STATE)
    y = (jnp.einsum('btsgj,bsgjp->btgjp', scores, xdt)
         + jnp.einsum('btgn,bgjpn->btgjp', cm, s_g) * grp(jnp.exp(acum))[..., None])
    w_end = grp(jnp.exp(acum[:, -1:] - acum))
    s_g = (s_g * grp(jnp.exp(acum[:, -1]))[..., None, None]
           + jnp.einsum('bsgn,bsgjp->bgjpn', bm, xdt * w_end[..., None]))
    return s_g.reshape(state.shape), y.reshape(x.shape)


def ssd_mixer(h_ctx, h_lat, w_in, conv_w, conv_b, dt_bias, a_log, d_skip, norm_w, w_out):
    def prep(h):
        bsz, length, _ = h.shape
        z, xbc, dt = jnp.split(h @ w_in, [SSD_INNER, SSD_INNER + SSD_CONV_DIM], axis=-1)
        xbc = jax.nn.silu(dwconv1d(xbc, conv_w, conv_b, SHORT_CONV_PAD)).astype(jnp.float32)
        xs, bm, cm = jnp.split(xbc, [SSD_INNER, SSD_INNER + SSD_BC], axis=-1)
        xs = xs.reshape(bsz, length, SSD_HEADS, SSD_HEAD_DIM)
        bm = bm.reshape(bsz, length, SSD_GROUPS, SSD_STATE)
        cm = cm.reshape(bsz, length, SSD_GROUPS, SSD_STATE)
        dt = jax.nn.softplus(dt.astype(jnp.float32).reshape(bsz, length, 2, SSD_HEADS) + dt_bias.astype(jnp.float32))
        da = dt * -jnp.exp(a_log.astype(jnp.float32))
        return (z, xs, (xs, dt[:, :, 0], da[:, :, 0], bm, cm), (xs, dt[:, :, 1], da[:, :, 1], bm, cm))
    z_ctx, x_ctx, ctx_f, ctx_b = prep(h_ctx)
    z_lat, x_lat, lat_f, lat_b = prep(h_lat)
    s0 = jnp.zeros((h_ctx.shape[0], SSD_HEADS, SSD_HEAD_DIM, SSD_STATE), jnp.float32)
    scan = functools.partial(chunked_scan, ssd_chunk_step, SSD_CHUNK)
    y_ctx, y_lat = run_bidirectional(scan, ctx_f, lat_f, ctx_b, lat_b, s0)
    def out(y, xs, z):
        y = (y + xs * d_skip.astype(jnp.float32)[:, None]).reshape(z.shape)
        y = rms_norm(y * jax.nn.silu(z.astype(jnp.float32))) * norm_w.astype(jnp.float32)
        return y.astype(z.dtype) @ w_out
    return out(y_ctx, x_ctx, z_ctx), out(y_lat, x_lat, z_lat)


def glu_down(u, w_down):
    a, v = jnp.split(u, 2, axis=-1)
    return (jax.nn.silu(a) * v) @ w_down


def conv_ffn_latent(h, w_up, conv_w, conv_b, w_down):
    bsz, length, _ = h.shape
    rows = length // GRID_W
    u = (h @ w_up).reshape(bsz, rows, GRID_W, 2 * FFN_HIDDEN)
    u = dwconv2d(u, conv_w, conv_b).reshape(bsz, length, 2 * FFN_HIDDEN)
    return glu_down(u, w_down)


def conv_ffn_context(h, w_up, conv_w, conv_b, w_down):
    u = dwconv1d(h @ w_up, conv_w[1], conv_b, (1, 1))
    return glu_down(u, w_down)


def setup_inputs(seed: int = 0) -> dict:
    key = jax.random.key(seed)
    keys = iter(jax.random.split(key, 48))
    def normal(shape, std):
        return std * jax.random.normal(next(keys), shape, jnp.float32)
    def uniform(shape, lo, hi):
        return jax.random.uniform(next(keys), shape, jnp.float32, lo, hi)
    n_lru, n_hgrn, n_ssd = [len(range(m, DEPTH, N_MIXERS)) for m in range(N_MIXERS)]
    d = D_MODEL
    lam_s = uniform((n_lru, 2, LRU_WIDTH), 0.9, 0.999) ** (1.0 / RG_C)
    dt0 = jnp.exp(uniform((n_ssd, 2, SSD_HEADS), math.log(1e-3), math.log(1e-1)))
    return {
        'x': normal((BATCH, SEQ, d), 1.0),
        'c': normal((BATCH, d), 1.0),
        'ctx': normal((BATCH, CTX_LEN, d), 1.0),
        'c_ctx': normal((d,), 1.0),
        'mod_w': normal((DEPTH, d, MOD_CHUNKS * d), 0.5 * d ** -0.5),
        'mod_b': normal((DEPTH, MOD_CHUNKS * d), 0.01),
        'ffn_w_up': normal((DEPTH, d, 2 * FFN_HIDDEN), d ** -0.5),
        'ffn_conv_w': normal((DEPTH, FFN_CONV, FFN_CONV, 2 * FFN_HIDDEN), 1.0 / FFN_CONV),
        'ffn_conv_b': normal((DEPTH, 2 * FFN_HIDDEN), 0.01),
        'ffn_w_down': normal((DEPTH, FFN_HIDDEN, d), FFN_HIDDEN ** -0.5),
        'lru_w_in': normal((n_lru, d, 2 * LRU_WIDTH), d ** -0.5),
        'lru_conv_w': normal((n_lru, SHORT_CONV_WIDTH, LRU_WIDTH), SHORT_CONV_WIDTH ** -0.5),
        'lru_conv_b': normal((n_lru, LRU_WIDTH), 0.01),
        'lru_gate_w': normal((n_lru, 2, 2, LRU_BLOCKS, LRU_BLOCK, LRU_BLOCK), LRU_BLOCK ** -0.5),
        'lru_gate_b': normal((n_lru, 2, 2, LRU_BLOCKS, LRU_BLOCK), 0.01),
        'lru_lambda': jnp.log(lam_s) - jnp.log1p(-lam_s),
        'lru_w_out': normal((n_lru, LRU_WIDTH, d), LRU_WIDTH ** -0.5),
        'hgrn_lb_logits': normal((DEPTH, 2, d), 0.1),
        'hgrn_w_in': normal((n_hgrn, d, 5 * d), d ** -0.5),
        'hgrn_norm_w': 1.0 + normal((n_hgrn, HGRN_HEAD_DIM), 0.02),
        'hgrn_w_out': normal((n_hgrn, d, d), d ** -0.5),
        'ssd_w_in': normal((n_ssd, d, SSD_INNER + SSD_CONV_DIM + 2 * SSD_HEADS), d ** -0.5),
        'ssd_conv_w': normal((n_ssd, SHORT_CONV_WIDTH, SSD_CONV_DIM), SHORT_CONV_WIDTH ** -0.5),
        'ssd_conv_b': normal((n_ssd, SSD_CONV_DIM), 0.01),
        'ssd_dt_bias': dt0 + jnp.log(-jnp.expm1(-dt0)),
        'ssd_a_log': jnp.log(uniform((n_ssd, 2, SSD_HEADS), 1.0, 16.0)),
        'ssd_d': 1.0 + normal((n_ssd, SSD_HEADS), 0.1),
        'ssd_norm_w': 1.0 + normal((n_ssd, SSD_INNER), 0.02),
        'ssd_w_out': normal((n_ssd, SSD_INNER, d), SSD_INNER ** -0.5),
        'final_norm_w': 1.0 + normal((d,), 0.02),
    }


def reference(x, c, ctx, c_ctx, mod_w, mod_b, ffn_w_up, ffn_conv_w, ffn_conv_b, ffn_w_down,
              lru_w_in, lru_conv_w, lru_conv_b, lru_gate_w, lru_gate_b, lru_lambda, lru_w_out,
              hgrn_lb_logits, hgrn_w_in, hgrn_norm_w, hgrn_w_out,
              ssd_w_in, ssd_conv_w, ssd_conv_b, ssd_dt_bias, ssd_a_log, ssd_d, ssd_norm_w, ssd_w_out,
              final_norm_w):
    lb_soft = jax.nn.softmax(hgrn_lb_logits.astype(jnp.float32), axis=0)
    hgrn_lb = jnp.cumsum(lb_soft, axis=0) - lb_soft[0]
    silu_c, silu_cc = jax.nn.silu(c), jax.nn.silu(c_ctx)
    x_lat, x_ctx = x, ctx
    for i in range(DEPTH):
        j, kind = i // N_MIXERS, i % N_MIXERS
        sh1, sc1, g1, sh2, sc2, g2 = jnp.split((silu_c @ mod_w[i] + mod_b[i])[:, None, :], MOD_CHUNKS, axis=-1)
        csh1, csc1, cg1, csh2, csc2, cg2 = jnp.split(silu_cc @ mod_w[i] + mod_b[i], MOD_CHUNKS, axis=-1)
        h_lat = modulate(rms_norm(x_lat), sh1, sc1)
        h_ctx = modulate(rms_norm(x_ctx), csh1, csc1)
        if kind == 0:
            o_ctx, o_lat = rglru_mixer(h_ctx, h_lat, lru_w_in[j], lru_conv_w[j], lru_conv_b[j],
                                       lru_gate_w[j], lru_gate_b[j], lru_lambda[j], lru_w_out[j])
        elif kind == 1:
            o_ctx, o_lat = hgrn2_mixer(h_ctx, h_lat, hgrn_lb[i], hgrn_w_in[j], hgrn_norm_w[j], hgrn_w_out[j])
        else:
            o_ctx, o_lat = ssd_mixer(h_ctx, h_lat, ssd_w_in[j], ssd_conv_w[j], ssd_conv_b[j], ssd_dt_bias[j],
                                     ssd_a_log[j], ssd_d[j], ssd_norm_w[j], ssd_w_out[j])
        x_lat = x_lat + g1 * o_lat
        x_lat = x_lat + g2 * conv_ffn_latent(modulate(rms_norm(x_lat), sh2, sc2),
                                             ffn_w_up[i], ffn_conv_w[i], ffn_conv_b[i], ffn_w_down[i])
        if i < DEPTH - 1:
            x_ctx = x_ctx + cg1 * o_ctx
            x_ctx = x_ctx + cg2 * conv_ffn_context(modulate(rms_norm(x_ctx), csh2, csc2),
                                                   ffn_w_up[i], ffn_conv_w[i], ffn_conv_b[i], ffn_w_down[i])
    return rms_norm(x_lat) * final_norm_w
```

```python
import numpy as np
import concourse.bass as bass
import concourse.mybir as mybir
from concourse.bass_utils import run_bass_kernel_spmd
from contextlib import ExitStack

F32 = mybir.dt.float32
ALU = mybir.AluOpType
AF = mybir.ActivationFunctionType
AX = mybir.AxisListType
ENG = ('pe', 'dve', 'act', 'pool', 'sp')
D = 1024
CTX = 256
EPS = 1e-6


class Buf:
    __slots__ = ('w', 'r')

    def __init__(self):
        self.w = None
        self.r = {}


class Tl:
    def __init__(self, t):
        self.t = t
        self.b = Buf()


class _Cap:
    def __getattr__(self, name):
        def f(*a, **k):
            return (name, a, k)
        return f


_CAP = _Cap()


class Sched:
    def __init__(self, nc, es, n_dma_sems=16):
        self.nc = nc
        self.prog = {e: [] for e in ENG}
        self.semobj = {}
        for e in ENG:
            self.semobj[e] = es.enter_context(nc.semaphore('s_' + e))
        self.cnt = {e: 0 for e in ENG}
        self.seen = {e: {} for e in ENG}
        self.nd = n_dma_sems
        for i in range(n_dma_sems):
            self.semobj[('d', i)] = es.enter_context(nc.semaphore('d%d' % i))
        self.dcnt = [0] * n_dma_sems
        self.dnext = 0
        self.self_sync = {'dve', 'act', 'pool'}

    def _wait(self, E, tok):
        key, val = tok
        if key == E and E not in self.self_sync:
            return
        if self.seen[E].get(key, 0) >= val:
            return
        self.seen[E][key] = val
        sem = self.semobj[key]
        self.prog[E].append(lambda eng, sem=sem, val=val: eng.wait_ge(sem, val))

    def _deps(self, E, reads, writes):
        for b in reads:
            if b.w is not None:
                self._wait(E, b.w)
        for b in writes:
            if b.w is not None:
                self._wait(E, b.w)
            for tok in b.r.values():
                self._wait(E, tok)

    def op(self, E, fn, reads=(), writes=()):
        reads = [x.b if isinstance(x, Tl) else x for x in reads]
        writes = [x.b if isinstance(x, Tl) else x for x in writes]
        self._deps(E, reads, writes)
        self.cnt[E] += 1
        n = self.cnt[E]
        sem = self.semobj[E]
        rec = fn(_CAP)
        self.prog[E].append(lambda eng, rec=rec, sem=sem: getattr(eng, rec[0])(*rec[1], **rec[2]).then_inc(sem, 1))
        tok = (E, n)
        for b in reads:
            b.r[E] = tok
        for b in writes:
            b.w = tok
            b.r = {}

    def dma(self, Q, out_ap, in_ap, reads=(), writes=()):
        reads = [x.b if isinstance(x, Tl) else x for x in reads]
        writes = [x.b if isinstance(x, Tl) else x for x in writes]
        i = self.dnext
        self.dnext = (i + 1) % self.nd
        key = ('d', i)
        if self.dcnt[i] > 0:
            self._wait(Q, (key, self.dcnt[i]))
        self._deps(Q, reads, writes)
        self.dcnt[i] += 16
        val = self.dcnt[i]
        sem = self.semobj[key]
        self.prog[Q].append(
            lambda eng, o=out_ap, a=in_ap, sem=sem: eng.dma_start(out=o, in_=a).then_inc(sem, 16))
        tok = (key, val)
        for b in reads:
            b.r[key] = tok
        for b in writes:
            b.w = tok
            b.r = {}

    def barrier(self):
        for E in ENG:
            for P in ENG:
                if P != E and self.cnt[P] > 0:
                    self._wait(E, (P, self.cnt[P]))
            for i in range(self.nd):
                if self.dcnt[i] > 0:
                    self._wait(E, (('d', i), self.dcnt[i]))

    def emit(self):
        with self.nc.Block() as block:
            @block.sync
            def _(e):
                for f in self.prog['sp']:
                    f(e)

            @block.tensor
            def _(e):
                for f in self.prog['pe']:
                    f(e)

            @block.vector
            def _(e):
                for f in self.prog['dve']:
                    f(e)

            @block.scalar
            def _(e):
                for f in self.prog['act']:
                    f(e)

            @block.gpsimd
            def _(e):
                for f in self.prog['pool']:
                    f(e)


def make_consts():
    c = {}
    idx = np.arange(128)
    c['ident'] = np.eye(128, dtype=np.float32)
    c['ones'] = np.ones((128, 128), np.float32)
    c['m1f'] = (idx[:, None] <= idx[None, :]).astype(np.float32)
    c['m1b'] = (idx[:, None] >= idx[None, :]).astype(np.float32)
    c['m2f'] = (idx[:, None] > idx[None, :]).astype(np.float32)
    c['m2b'] = (idx[:, None] < idx[None, :]).astype(np.float32)
    same = (idx[:, None] // 32) == (idx[None, :] // 32)
    c['bdf'] = (same & (idx[:, None] <= idx[None, :])).astype(np.float32)
    c['bdb'] = (same & (idx[:, None] >= idx[None, :])).astype(np.float32)
    rm = np.zeros((128, 128), np.float32)
    for k in range(4):
        rm[32 * k:32 * k + 32, k] = 1.0
    c['rowmask'] = rm
    m01 = np.ones((128, 512), np.float32)
    m01[:, ::32] = 0.0
    names = ['ident', 'ones', 'm1f', 'm1b', 'm2f', 'm2b', 'bdf', 'bdb', 'rowmask']
    arr = np.concatenate([c[n] for n in names] + [m01], axis=1)
    offs = {n: i * 128 for i, n in enumerate(names)}
    offs['m01'] = len(names) * 128
    return np.ascontiguousarray(arr), offs


def chunk_w(w, kc=None):
    K, N = w.shape
    assert K % 128 == 0 and N % 128 == 0
    return np.ascontiguousarray(w.reshape(K // 128, 128, N // 128, 128).transpose(2, 1, 0, 3))


def fm_vec(v):
    sh = v.shape
    n = sh[-1] // 128
    a = v.reshape(sh[:-1] + (n, 128))
    a = np.moveaxis(a, -1, 0)
    return np.ascontiguousarray(a)


class Prog:
    def __init__(self, L, kinds, depth_total):
        self.L = L
        self.LT = CTX + L
        self.kinds = kinds
        self.depth = len(kinds)
        self.nc = bass.Bass("TRN2", target_bir_lowering=False)
        self.inputs = {}
        self.coffs = None
        self.debug = False
        self.dbg_out = {}

    def din(self, name, shape):
        t = self.nc.dram_tensor(name, list(shape), F32, kind="ExternalInput").ap()
        self.inputs[name] = t
        return t

    def dscr(self, name, shape):
        return self.nc.dram_tensor(name, list(shape), F32, kind="Internal").ap()

    def sb(self, es, name, shape):
        self.uid = getattr(self, 'uid', 0) + 1
        return Tl(es.enter_context(self.nc.sbuf_tensor('%s_%d' % (name, self.uid), list(shape), F32)))

    def ps(self, es, name, shape):
        self.uid = getattr(self, 'uid', 0) + 1
        return Tl(es.enter_context(self.nc.psum_tensor('%s_%d' % (name, self.uid), list(shape), F32)))

    def dbufs(self, name, lo, hi):
        d = self.dram_b.setdefault(name, {})
        out = []
        for g in range(lo // 256, (hi - 1) // 256 + 1):
            if g not in d:
                d[g] = Buf()
            out.append(d[g])
        return out

    def tiles(self, T=512):
        tl = [dict(t0=0, T=CTX, s0=0, s1=CTX, ctx=True)]
        for i in range(self.L // T):
            tl.append(dict(t0=CTX + i * T, T=T, s0=CTX, s1=self.LT, ctx=False))
        return tl

    def wload(self, ap_chunk, KC):
        S = self.S
        t = self.wring[self.wnext]
        self.wnext = (self.wnext + 1) % len(self.wring)
        S.dma('sp', t.t[:, :KC, :], ap_chunk, writes=[t])
        return t

    def mm(self, out_ap, OUT, pairs, reads):
        n = len(pairs)
        for i, (l, r) in enumerate(pairs):
            self.S.op('pe', lambda e, l=l, r=r, i=i: e.matmul(out_ap, l, r, start=(i == 0), stop=(i == n - 1)),
                      reads, [OUT])

    def proj_fm(self, pt, W, wt, h, c0=0):
        for a in range(0, W, 512):
            b = min(W, a + 512)
            self.mm(pt.t[:, a:b], pt, [(wt.t[:, kc, :], h.t[:, kc, c0 + a:c0 + b]) for kc in range(8)], [wt, h])

    def load_norm(self, Xin, tile, hl, hr, l, which):
        S = self.S
        xt, h, sq, rstd, pn = self.xt, self.h, self.sq, self.rstd, self.pn
        t0, T = tile['t0'], tile['T']
        a, b = t0 - hl, t0 + T + hr
        W = b - a
        lo, hi = max(a, tile['s0']), min(b, tile['s1'])
        src = self.dr[Xin].rearrange("c p t -> p c t")[:, :, lo:hi]
        S.dma('sp', xt.t[:, :, lo - a:hi - a], src, reads=self.dbufs(Xin, lo, hi), writes=[xt])
        if lo > a:
            S.op('pool', lambda e: e.memset(xt.t[:, :, 0:lo - a], 1.0), [], [xt])
        if hi < b:
            S.op('pool', lambda e: e.memset(xt.t[:, :, hi - a:W], 1.0), [], [xt])
        col = 1 if tile['ctx'] else 0
        base = 0 if which == 1 else 24
        for c in range(8):
            s = sq[c % 2]
            S.op('act', lambda e, s=s, c=c: e.activation(s.t[:, :W], xt.t[:, c, :W], AF.Square), [xt], [s])
            for a2 in range(0, W, 512):
                b2 = min(W, a2 + 512)
                S.op('pe', lambda e, s=s, c=c, a2=a2, b2=b2: e.matmul(pn.t[:, a2:b2], self.cst('ones'), s.t[:, a2:b2],
                                                                    start=(c == 0), stop=(c == 7)), [s, self.CT], [pn])
        S.op('dve', lambda e: e.tensor_scalar(rstd.t[:, :W], pn.t[:, :W], 1.0 / D, EPS, ALU.mult, ALU.add), [pn], [rstd])
        S.op('act', lambda e: e.activation(rstd.t[:, :W], rstd.t[:, :W], AF.Sqrt), [rstd], [rstd])
        S.op('dve', lambda e: e.reciprocal(rstd.t[:, :W], rstd.t[:, :W]), [rstd], [rstd])
        for c in range(8):
            S.op('dve', lambda e, c=c: e.tensor_tensor(h.t[:, c, :W], xt.t[:, c, :W], rstd.t[:, :W], ALU.mult), [xt, rstd], [h])
            S.op('act', lambda e, c=c: e.activation(h.t[:, c, :W], h.t[:, c, :W], AF.Identity,
                                                     bias=self.modv.t[:, l, base + c, col:col + 1],
                                                     scale=self.modv.t[:, l, base + 8 + c, col:col + 1]), [h, self.modv], [h])
        if lo > a:
            S.op('pool', lambda e: e.memset(h.t[:, :, 0:lo - a], 0.0), [], [h])
        if hi < b:
            S.op('pool', lambda e: e.memset(h.t[:, :, hi - a:W], 0.0), [], [h])
        return W

    def dbg(self, name, tl, ap, shape):
        if not self.debug or name in self.dbg_out:
            return
        d = self.nc.dram_tensor('dbg_' + name, list(shape), F32, kind="ExternalOutput").ap()
        self.dbg_out[name] = d
        self.S.dma('sp', d, ap, reads=[tl], writes=[Buf()])

    def cst(self, name, n=128):
        o = self.coffs[name]
        return self.CT.t[:, o:o + n]

    def out_proj(self, Xout, tile, l, wdram, KC, G, gbase, hl, do_store=True):
        S = self.S
        T, t0 = tile['T'], tile['t0']
        col = 1 if tile['ctx'] else 0
        for oc in range(8):
            wt = self.wload(wdram[oc], KC)
            po = self.po[oc % 2]
            self.mm(po.t[:, :T], po, [(wt.t[:, k, :], G.t[:, k, :T]) for k in range(KC)], [wt, G])
            S.op('dve', lambda e, oc=oc, po=po: e.scalar_tensor_tensor(
                self.xo.t[:, oc, :T], po.t[:, :T], self.modv.t[:, l, gbase + oc, col:col + 1],
                self.xt.t[:, oc, hl:hl + T], ALU.mult, ALU.add), [po, self.xt, self.modv], [self.xo])
        dst = self.dr[Xout].rearrange("c p t -> p c t")[:, :, t0:t0 + T]
        S.dma('sp', dst, self.xo.t[:, :, :T], reads=[self.xo], writes=self.dbufs(Xout, t0, t0 + T))

    def build(self):
        nc = self.nc
        L, LT, depth = self.L, self.LT, self.depth
        kinds = self.kinds
        n_lru = max(1, sum(1 for k in kinds if k == 0))
        cst_np, self.coffs = make_consts()
        self.cst_np = cst_np
        dr = self.dr = {}
        dr['X0'] = self.din('xin', [8, 128, LT])
        dr['cvec'] = self.din('cvec', [128, 8, 2])
        dr['consts'] = self.din('consts', list(cst_np.shape))
        dr['mod_w'] = self.din('mod_w', [depth, 48, 128, 8, 128])
        dr['mod_b'] = self.din('mod_b', [128, depth, 48])
        dr['w_up'] = self.din('w_up', [depth, 44, 128, 8, 128])
        dr['cw'] = self.din('cw', [128, depth, 44, 9])
        dr['cb'] = self.din('cb', [128, depth, 44])
        dr['w_down'] = self.din('w_down', [depth, 8, 128, 22, 128])
        dr['lru_w_in'] = self.din('lru_w_in', [n_lru, 20, 128, 8, 128])
        dr['lru_cw'] = self.din('lru_cw', [128, n_lru, 10, 4])
        dr['lru_cb'] = self.din('lru_cb', [128, n_lru, 10])
        dr['lru_gw'] = self.din('lru_gw', [n_lru, 2, 128, 2, 10, 128])
        dr['lru_gb'] = self.din('lru_gb', [128, n_lru, 2, 2, 10])
        dr['lru_lam'] = self.din('lru_lam', [128, n_lru, 2, 10])
        dr['lru_w_out'] = self.din('lru_w_out', [n_lru, 8, 128, 10, 128])
        dr['fnw'] = self.din('fnw', [128, 8])
        dr['ssd_w_in'] = self.din('ssd_w_in', [49, 128, 8, 128])
        dr['ssd_cw'] = self.din('ssd_cw', [128, 32, 4])
        dr['ssd_cb'] = self.din('ssd_cb', [128, 32])
        dr['ssd_dtb'] = self.din('ssd_dtb', [128, 64])
        dr['ssd_alog'] = self.din('ssd_alog', [128, 64])
        dr['ssd_dsk'] = self.din('ssd_dsk', [128, 16])
        dr['ssd_nw'] = self.din('ssd_nw', [128, 16])
        dr['ssd_w_out'] = self.din('ssd_w_out', [8, 128, 16, 128])
        dr['hgrn_w_in'] = self.din('hgrn_w_in', [1, 40, 128, 8, 128])
        dr['hgrn_lg'] = self.din('hgrn_lg', [128, 4, 2, 8])
        dr['hgrn_nw'] = self.din('hgrn_nw', [128, 1])
        dr['hgrn_w_out'] = self.din('hgrn_w_out', [1, 8, 128, 8, 128])
        dr['XA'] = self.dscr('XA', [8, 128, LT])
        dr['XB'] = self.dscr('XB', [8, 128, LT])
        dr['SF'] = self.dscr('SF', [16, 128, LT])
        dr['out'] = nc.dram_tensor('yout', [8, 128, L], F32, kind="ExternalOutput").ap()
        self.dram_b = {}

        es = ExitStack()
        with es:
            S = self.S = Sched(nc, es)
            self.CT = self.sb(es, 'consts_s', list(cst_np.shape))
            S.dma('sp', self.CT.t[:], dr['consts'], writes=[self.CT])
            self.modv = self.sb(es, 'modv', [128, depth, 48, 2])
            self.wring = [self.sb(es, 'wr%d' % i, [128, 22, 128]) for i in range(3)]
            self.wnext = 0
            self.xt = self.sb(es, 'xt', [128, 8, 640])
            self.h = self.sb(es, 'h', [128, 8, 640])
            self.sq = [self.sb(es, 'sq%d' % i, [128, 640]) for i in range(2)]
            self.rstd = self.sb(es, 'rstd', [128, 640])
            self.xo = self.sb(es, 'xo', [128, 8, 512])
            self.phase_mod(es)
            xin = 'X0'
            nl = 0
            for l, kind in enumerate(kinds):
                last = (l == depth - 1)
                if kind == 0:
                    self.phase_lru(l, nl, xin, 'XB', last)
                    nl += 1
                elif kind == 1:
                    self.phase_hgrn(l, 0, xin, 'XB', last)
                else:
                    self.phase_ssd(l, xin, 'XB', last)
                S.barrier()
                self.phase_ffn(l, 'XB', 'XA', last)
                S.barrier()
                xin = 'XA'
            self.phase_final('XA')
            S.barrier()
            S.emit()
        return nc

    def phase_mod(self, es_outer):
        S = self.S
        dr = self.dr
        with ExitStack() as es:
            cv = self.sb(es, 'cv', [128, 8, 2])
            mb = self.sb(es, 'mb', [128, self.depth, 48])
            pm = [self.ps(es, 'pm%d' % i, [128, 512]) for i in range(2)]
            S.dma('sp', cv.t[:], dr['cvec'], writes=[cv])
            S.dma('sp', mb.t[:], dr['mod_b'], writes=[mb])
            S.op('act', lambda e: e.activation(cv.t[:], cv.t[:], AF.Silu), [cv], [cv])
            for l in range(self.depth):
                for j in range(48):
                    wt = self.wload(dr['mod_w'][l, j], 8)
                    p = pm[j % 2]
                    self.mm(p.t[:, 0:2], p, [(wt.t[:, kc, :], cv.t[:, kc, :]) for kc in range(8)], [wt, cv])
                    S.op('act', lambda e, p=p, l=l, j=j: e.activation(self.modv.t[:, l, j, :], p.t[:, 0:2], AF.Identity,
                                                                      bias=mb.t[:, l, j:j + 1]), [p, mb], [self.modv])
                for base in (8, 32):
                    S.op('dve', lambda e, l=l, base=base: e.tensor_scalar(
                        self.modv.t[:, l, base:base + 8, :], self.modv.t[:, l, base:base + 8, :], 1.0, None, ALU.add),
                        [self.modv], [self.modv])
            S.barrier()

    def phase_ffn(self, l, Xin, Xout, last):
        S = self.S
        dr = self.dr
        with ExitStack() as es:
            self.pn = self.ps(es, 'pn', [128, 1024])
            pu = [self.ps(es, 'pu%d' % i, [128, 1024]) for i in range(2)]
            self.po = [self.ps(es, 'po%d' % i, [128, 512]) for i in range(2)]
            cw = self.sb(es, 'cw', [128, 44, 9])
            cb = self.sb(es, 'cb', [128, 44])
            S.dma('sp', cw.t[:], dr['cw'][:, l], writes=[cw])
            S.dma('sp', cb.t[:], dr['cb'][:, l], writes=[cb])
            U = [self.sb(es, 'U%d' % i, [128, 10, 66]) for i in range(2)]
            Uc = [self.sb(es, 'Uc%d' % i, [128, 264]) for i in range(2)]
            acc = [self.sb(es, 'acc%d' % i, [128, 512]) for i in range(2)]
            G = self.sb(es, 'G', [128, 22, 512])
            for u in U:
                S.op('pool', lambda e, u=u: e.memset(u.t[:], 0.0), [], [u])
            for tile in self.tiles():
                if tile['ctx'] and last:
                    continue
                T = tile['T']
                hl = 1 if tile['ctx'] else 64
                W = self.load_norm(Xin, tile, hl, hl, l, 2)
                for j in range(22):
                    for half, jj in ((0, j), (1, 22 + j)):
                        eng = 'dve'
                        wt = self.wload(dr['w_up'][l, jj], 8)
                        p = pu[half]
                        self.proj_fm(p, W, wt, self.h)
                        ac = acc[half]
                        if tile['ctx']:
                            u = Uc[half]
                            S.op('act', lambda e, u=u, p=p: e.activation(u.t[:, 0:W], p.t[:, 0:W], AF.Identity), [p], [u])
                            taps = [(3 + dc, u.t[:, dc:dc + T]) for dc in range(3)]
                            accv = ac.t[:, :T]
                        else:
                            u = U[half]
                            S.op('act', lambda e, u=u, p=p: e.activation(
                                u.t[:, :, 1:65], p.t[:, 0:640].rearrange("p (r c) -> p r c", c=64), AF.Identity), [p], [u])
                            taps = [(dr_ * 3 + dc, u.t[:, dr_:dr_ + 8, dc:dc + 64]) for dr_ in range(3) for dc in range(3)]
                            accv = ac.t[:, :T].rearrange("p (r c) -> p r c", c=64)
                        for ti, (tap, src) in enumerate(taps):
                            if ti == 0:
                                S.op(eng, lambda e, src=src, tap=tap, jj=jj, accv=accv: e.tensor_scalar(
                                    accv, src, cw.t[:, jj, tap:tap + 1], cb.t[:, jj:jj + 1], ALU.mult, ALU.add),
                                    [u, cw, cb], [ac])
                            else:
                                S.op(eng, lambda e, src=src, tap=tap, jj=jj, accv=accv: e.scalar_tensor_tensor(
                                    accv, src, cw.t[:, jj, tap:tap + 1], accv, ALU.mult, ALU.add), [u, cw, ac], [ac])
                    S.op('act', lambda e: e.activation(acc[0].t[:, :T], acc[0].t[:, :T], AF.Silu), [acc[0]], [acc[0]])
                    S.op('dve', lambda e, j=j: e.tensor_tensor(G.t[:, j, :T], acc[0].t[:, :T], acc[1].t[:, :T], ALU.mult),
                         [acc[0], acc[1]], [G])
                dump = (tile['t0'] == CTX and l == 0)
                if dump:
                    self.dbg('h2', self.h, self.h.t[:, :, :W], [128, 8, W])
                    self.dbg('G', G, G.t[:], [128, 22, 512])
                self.out_proj(Xout, tile, l, dr['w_down'][l], 22, G, 40, hl)
                if dump:
                    self.dbg('xo2', self.xo, self.xo.t[:], [128, 8, 512])

    def phase_lru(self, l, j, Xin, Xout, last):
        S = self.S
        dr = self.dr
        with ExitStack() as es:
            self.pn = self.ps(es, 'pn', [128, 1024])
            px = self.ps(es, 'px', [128, 1024])
            pg = [self.ps(es, 'pg%d' % i, [128, 512]) for i in range(2)]
            self.po = [self.ps(es, 'po%d' % i, [128, 512]) for i in range(2)]
            cw = self.sb(es, 'lcw', [128, 10, 4])
            cb = self.sb(es, 'lcb', [128, 10])
            gb = self.sb(es, 'lgb', [128, 2, 2, 10])
            cl = self.sb(es, 'lcl', [128, 2, 10])
            gw = self.sb(es, 'lgw', [128, 2, 10, 128])
            hst = self.sb(es, 'hst', [128, 10])
            S.dma('sp', cw.t[:], dr['lru_cw'][:, j], writes=[cw])
            S.dma('sp', cb.t[:], dr['lru_cb'][:, j], writes=[cb])
            S.dma('sp', gb.t[:], dr['lru_gb'][:, j], writes=[gb])
            S.dma('sp', cl.t[:], dr['lru_lam'][:, j], writes=[cl])
            S.op('act', lambda e: e.activation(cl.t[:], cl.t[:], AF.Exp, scale=-1.0), [cl], [cl])
            S.op('dve', lambda e: e.tensor_scalar(cl.t[:], cl.t[:], 1.0, None, ALU.add), [cl], [cl])
            S.op('act', lambda e: e.activation(cl.t[:], cl.t[:], AF.Ln), [cl], [cl])
            S.op('dve', lambda e: e.tensor_scalar(cl.t[:], cl.t[:], -8.0, None, ALU.mult), [cl], [cl])
            names = ['xb', 'xc', 'rr', 'ii', 'aa', 't1', 'bb', 'hs', 'hf', 'gy']
            tl = {n: self.sb(es, 'l_' + n, [128, 520]) for n in names}
            GR = self.sb(es, 'GR', [128, 10, 512])
            lat = self.tiles()
            for d in range(2):
                S.dma('sp', gw.t[:], dr['lru_gw'][j, d], writes=[gw])
                S.op('pool', lambda e: e.memset(hst.t[:], 0.0), [], [hst])
                order = lat if d == 0 else [lat[0]] + lat[:0:-1]
                for tile in order:
                    T, t0 = tile['T'], tile['t0']
                    W = self.load_norm(Xin, tile, 2, 1, l, 1)
                    need_out = (d == 1) and not (tile['ctx'] and last)
                    dump = (tile['t0'] == CTX and l == 0)
                    if dump and d == 0:
                        self.dbg('modv', self.modv, self.modv.t[:, 0, :, :], [128, 48, 2])
                        self.dbg('h1', self.h, self.h.t[:, :, :W], [128, 8, W])
                    for blk in range(10):
                        xb, xc, rr, ii, aa, t1, bb, hs, hf, gy = [tl[n] for n in names]
                        wt = self.wload(dr['lru_w_in'][j, 10 + blk], 8)
                        self.proj_fm(px, W, wt, self.h)
                        S.op('act', lambda e: e.activation(xb.t[:, :W], px.t[:, :W], AF.Identity), [px], [xb])
                        for k in range(4):
                            if k == 0:
                                S.op('dve', lambda e, blk=blk: e.tensor_scalar(
                                    xc.t[:, :T], xb.t[:, 0:T], cw.t[:, blk, 0:1], cb.t[:, blk:blk + 1], ALU.mult, ALU.add),
                                    [xb, cw, cb], [xc])
                            else:
                                S.op('dve', lambda e, blk=blk, k=k: e.scalar_tensor_tensor(
                                    xc.t[:, :T], xb.t[:, k:k + T], cw.t[:, blk, k:k + 1], xc.t[:, :T], ALU.mult, ALU.add),
                                    [xb, cw, xc], [xc])
                        for g, dst in ((0, rr), (1, ii)):
                            p = pg[g]
                            self.mm(p.t[:, :T], p, [(gw.t[:, g, blk, :], xc.t[:, :T])], [gw, xc])
                            S.op('act', lambda e, p=p, dst=dst, g=g, blk=blk: e.activation(
                                dst.t[:, :T], p.t[:, :T], AF.Sigmoid, bias=gb.t[:, d, g, blk:blk + 1]), [p, gb], [dst])
                        S.op('act', lambda e, blk=blk: e.activation(aa.t[:, :T], rr.t[:, :T], AF.Exp,
                                                                     scale=cl.t[:, d, blk:blk + 1]), [rr, cl], [aa])
                        S.op('pool', lambda e: e.tensor_tensor(t1.t[:, :T], aa.t[:, :T], aa.t[:, :T], ALU.mult), [aa], [t1])
                        S.op('pool', lambda e: e.tensor_scalar(t1.t[:, :T], t1.t[:, :T], -1.0, 1.0, ALU.mult, ALU.add), [t1], [t1])
                        S.op('act', lambda e: e.activation(t1.t[:, :T], t1.t[:, :T], AF.Sqrt), [t1], [t1])
                        S.op('pool', lambda e: e.tensor_tensor(bb.t[:, :T], ii.t[:, :T], xc.t[:, :T], ALU.mult), [ii, xc], [bb])
                        S.op('dve', lambda e: e.tensor_tensor(bb.t[:, :T], bb.t[:, :T], t1.t[:, :T], ALU.mult), [bb, t1], [bb])
                        if dump and blk == 0:
                            for nm, tt in (('xb', xb), ('xc', xc), ('rr', rr), ('ii', ii), ('aa', aa), ('bb', bb)):
                                self.dbg('%s%d' % (nm, d), tt, tt.t[:, :512], [128, 512])
                        if d == 0:
                            S.op('dve', lambda e, blk=blk: e.tensor_tensor_scan(
                                hs.t[:, :T], aa.t[:, :T], bb.t[:, :T], hst.t[:, blk:blk + 1], ALU.mult, ALU.add),
                                [aa, bb, hst], [hs])
                            S.op('act', lambda e, blk=blk: e.activation(hst.t[:, blk:blk + 1], hs.t[:, T - 1:T], AF.Identity),
                                 [hs], [hst])
                            if dump and blk == 0:
                                self.dbg('hs0', hs, hs.t[:, :512], [128, 512])
                            dst = dr['SF'][blk, :, t0:t0 + T]
                            S.dma('sp', dst, hs.t[:, :T], reads=[hs], writes=self.dbufs('SF%d' % blk, t0, t0 + T))
                        else:
                            S.op('dve', lambda e, blk=blk: e.tensor_tensor_scan(
                                hs.t[:, T - 1::-1] if False else hs.t[:, 0:T][:, ::-1], aa.t[:, 0:T][:, ::-1], bb.t[:, 0:T][:, ::-1],
                                hst.t[:, blk:blk + 1], ALU.mult, ALU.add), [aa, bb, hst], [hs])
                            S.op('act', lambda e, blk=blk: e.activation(hst.t[:, blk:blk + 1], hs.t[:, 0:1], AF.Identity),
                                 [hs], [hst])
                            if dump and blk == 0:
                                self.dbg('hs1', hs, hs.t[:, :512], [128, 512])
                            if need_out:
                                S.dma('sp', hf.t[:, :T], dr['SF'][blk, :, t0:t0 + T],
                                      reads=self.dbufs('SF%d' % blk, t0, t0 + T), writes=[hf])
                                S.op('pool', lambda e: e.tensor_tensor(hs.t[:, :T], hs.t[:, :T], hf.t[:, :T], ALU.add), [hs, hf], [hs])
                                wt2 = self.wload(dr['lru_w_in'][j, blk], 8)
                                p = pg[0]
                                self.mm(p.t[:, :T], p, [(wt2.t[:, kc, :], self.h.t[:, kc, 2:2 + T]) for kc in range(8)], [wt2, self.h])
                                S.op('act', lambda e, p=p: e.activation(gy.t[:, :T], p.t[:, :T], AF.Gelu_apprx_tanh), [p], [gy])
                                S.op('dve', lambda e, blk=blk: e.tensor_tensor(GR.t[:, blk, :T], gy.t[:, :T], hs.t[:, :T], ALU.mult),
                                     [gy, hs], [GR])
                    if need_out:
                        if dump:
                            self.dbg('GR', GR, GR.t[:], [128, 10, 512])
                        self.out_proj(Xout, tile, l, dr['lru_w_out'][j], 10, GR, 16, 2)
                        if dump:
                            self.dbg('xo1', self.xo, self.xo.t[:], [128, 8, 512])
                    elif d == 1:
                        pass


    def phase_hgrn(self, l, j, Xin, Xout, last):
        S = self.S
        dr = self.dr
        with ExitStack() as es:
            self.pn = self.ps(es, 'pn', [128, 512])
            pp = [self.ps(es, 'pp%d' % i, [128, 512]) for i in range(2)]
            self.po = pp
            pv = self.ps(es, 'pv', [128, 512])
            pS = self.ps(es, 'pS', [128, 512])
            pT = self.ps(es, 'pT', [128, 512])
            pO = self.ps(es, 'pO', [128, 512])
            pU = self.ps(es, 'pU', [128, 512])
            lg = self.sb(es, 'lg', [128, 4, 2, 8])
            lb = self.sb(es, 'lb', [128, 2, 8])
            om = self.sb(es, 'om', [128, 2, 8])
            den = self.sb(es, 'den', [128, 2, 8])
            nw = self.sb(es, 'nw', [128, 1])
            S.dma('sp', lg.t[:], dr['hgrn_lg'], writes=[lg])
            S.dma('sp', nw.t[:], dr['hgrn_nw'][:, j:j + 1], writes=[nw])
            S.op('act', lambda e: e.activation(lg.t[:], lg.t[:], AF.Exp), [lg], [lg])
            S.op('dve', lambda e: e.tensor_tensor(den.t[:], lg.t[:, 0], lg.t[:, 1], ALU.add), [lg], [den])
            for k in (2, 3):
                S.op('dve', lambda e, k=k: e.tensor_tensor(den.t[:], den.t[:], lg.t[:, k], ALU.add), [lg, den], [den])
            S.op('dve', lambda e: e.reciprocal(den.t[:], den.t[:]), [den], [den])
            S.op('dve', lambda e: e.memset(lb.t[:], 0.0), [], [lb])
            for k in range(1, l + 1):
                S.op('dve', lambda e, k=k: e.tensor_tensor(lb.t[:], lb.t[:], lg.t[:, k], ALU.add), [lg, lb], [lb])
            S.op('dve', lambda e: e.tensor_tensor(lb.t[:], lb.t[:], den.t[:], ALU.mult), [lb, den], [lb])
            S.op('dve', lambda e: e.tensor_scalar(om.t[:], lb.t[:], -1.0, 1.0, ALU.mult, ALU.add), [lb], [om])
            names = ['q', 'fg', 'lf', 'kk', 'gc', 'G', 'ep', 'em', 'qd', 'ki', 'ke', 'O', 'of', 'sq']
            tl = {n: self.sb(es, 'g_' + n, [128, 512]) for n in names}
            VT = self.sb(es, 'VT', [128, 4, 5, 128])
            sm = self.sb(es, 'sm', [128, 128])
            keTs = self.sb(es, 'keTs', [128, 128])
            St = [[self.sb(es, 'St%d_%d' % (hd, i), [128, 128]) for i in range(2)] for hd in range(8)]
            OG = self.sb(es, 'OG', [128, 8, 512])
            lat = self.tiles()
            for d in range(2):
                cur = [0] * 8
                for hd in range(8):
                    S.op('pool', lambda e, hd=hd: e.memset(St[hd][0].t[:], 0.0), [], [St[hd][0]])
                order = lat if d == 0 else [lat[0]] + lat[:0:-1]
                bd = self.cst('bdf') if d == 0 else self.cst('bdb')
                for tile in order:
                    T, t0 = tile['T'], tile['t0']
                    nb = T // 128
                    ncnk = T // 32
                    self.load_norm(Xin, tile, 0, 0, l, 1)
                    need_out = (d == 1) and not (tile['ctx'] and last)
                    h = self.h
                    for hd in range(8):
                        q, fg, lf, kk, gc, G, ep, em, qd, ki, ke, O, of, sq = [tl[n] for n in names]
                        wt = self.wload(dr['hgrn_w_in'][j, hd], 8)
                        self.proj_fm(pp[0], T, wt, h)
                        S.op('act', lambda e: e.activation(q.t[:, :T], pp[0].t[:, :T], AF.Silu), [pp[0]], [q])
                        wt = self.wload(dr['hgrn_w_in'][j, 16 + 8 * d + hd], 8)
                        self.proj_fm(pp[1], T, wt, h)
                        S.op('act', lambda e: e.activation(fg.t[:, :T], pp[1].t[:, :T], AF.Sigmoid), [pp[1]], [fg])
                        S.op('dve', lambda e: e.tensor_scalar(fg.t[:, :T], fg.t[:, :T], om.t[:, d, hd:hd + 1], lb.t[:, d, hd:hd + 1],
                                                              ALU.mult, ALU.add), [fg, om, lb], [fg])
                        S.op('act', lambda e: e.activation(lf.t[:, :T], fg.t[:, :T], AF.Ln), [fg], [lf])
                        S.op('pool', lambda e: e.tensor_scalar(kk.t[:, :T], fg.t[:, :T], -1.0, 1.0, ALU.mult, ALU.add), [fg], [kk])
                        S.op('dve', lambda e: e.tensor_tensor_scan(gc.t[:, :T], self.cst('m01', 512)[:, :T], lf.t[:, :T], 0.0,
                                                                   ALU.mult, ALU.add), [lf, self.CT], [gc])
                        v3 = lambda t_: t_.t[:, :T].rearrange("p (c j) -> p c j", j=32)
                        if d == 0:
                            G = gc
                        else:
                            S.op('dve', lambda e: e.tensor_tensor(v3(G), v3(gc), v3(gc)[:, :, 31:32].to_broadcast([128, ncnk, 32]),
                                                                  ALU.subtract), [gc], [G])
                            S.op('dve', lambda e: e.tensor_tensor(G.t[:, :T], lf.t[:, :T], G.t[:, :T], ALU.subtract), [lf, G], [G])
                        S.op('act', lambda e: e.activation(ep.t[:, :T], G.t[:, :T], AF.Exp), [G], [ep])
                        S.op('act', lambda e: e.activation(em.t[:, :T], G.t[:, :T], AF.Exp, scale=-1.0), [G], [em])
                        S.op('dve', lambda e: e.tensor_tensor(qd.t[:, :T], q.t[:, :T], ep.t[:, :T], ALU.mult), [q, ep], [qd])
                        S.op('pool', lambda e: e.tensor_tensor(ki.t[:, :T], kk.t[:, :T], em.t[:, :T], ALU.mult), [kk, em], [ki])
                        eo = 31 if d == 0 else 0
                        S.op('dve', lambda e: e.tensor_tensor(v3(ke), v3(ki), v3(ep)[:, :, eo:eo + 1].to_broadcast([128, ncnk, 32]),
                                                              ALU.mult), [ki, ep], [ke])
                        wtv = self.wload(dr['hgrn_w_in'][j, 8 + hd], 8)
                        for blk in range(nb):
                            self.mm(pv.t[:, :128], pv, [(h.t[:, kc, blk * 128:(blk + 1) * 128], wtv.t[:, kc, :]) for kc in range(8)],
                                    [h, wtv])
                            S.op('act', lambda e, blk=blk: e.activation(VT.t[:, blk, 0, :], pv.t[:, :128], AF.Identity), [pv], [VT])
                            for c in range(4):
                                S.op('act', lambda e, blk=blk, c=c: e.activation(
                                    VT.t[:, blk, 1 + c, :], pv.t[:, :128], AF.Identity,
                                    scale=self.cst('rowmask')[:, c:c + 1]), [pv, self.CT], [VT])
                        for blk in (range(nb) if d == 0 else range(nb - 1, -1, -1)):
                            c0 = blk * 128
                            self.mm(pS.t[:, :128], pS, [(ki.t[:, c0:c0 + 128], qd.t[:, c0:c0 + 128])], [ki, qd])
                            S.op('dve', lambda e: e.tensor_tensor(sm.t[:], pS.t[:, :128], bd, ALU.mult), [pS, self.CT], [sm])
                            self.mm(pT.t[:, :128], pT, [(ke.t[:, c0:c0 + 128], self.cst('ident'))], [ke, self.CT])
                            S.op('act', lambda e: e.activation(keTs.t[:], pT.t[:, :128], AF.Identity), [pT], [keTs])
                            for c in (range(4) if d == 0 else range(3, -1, -1)):
                                cc = c0 + 32 * c
                                Sc = St[hd][cur[hd]]
                                Sn = St[hd][1 - cur[hd]]
                                cur[hd] = 1 - cur[hd]
                                self.mm(pO.t[:, 32 * c:32 * c + 32], pO,
                                        [(VT.t[:, blk, 0, :], sm.t[:, 32 * c:32 * c + 32]), (Sc.t[:], qd.t[:, cc:cc + 32])],
                                        [VT, sm, Sc, qd])
                                self.mm(pU.t[:, :128], pU, [(keTs.t[:], VT.t[:, blk, 1 + c, :])], [keTs, VT])
                                S.op('dve', lambda e, Sc=Sc, Sn=Sn, cc=cc: e.scalar_tensor_tensor(
                                    Sn.t[:], Sc.t[:], ep.t[:, cc + eo:cc + eo + 1], pU.t[:, :128], ALU.mult, ALU.add),
                                    [Sc, ep, pU], [Sn])
                            S.op('act', lambda e, c0=c0: e.activation(O.t[:, c0:c0 + 128], pO.t[:, :128], AF.Identity), [pO], [O])
                        if d == 0:
                            S.dma('sp', dr['SF'][hd, :, t0:t0 + T], O.t[:, :T], reads=[O], writes=self.dbufs('SF%d' % hd, t0, t0 + T))
                        elif need_out:
                            S.dma('sp', of.t[:, :T], dr['SF'][hd, :, t0:t0 + T], reads=self.dbufs('SF%d' % hd, t0, t0 + T), writes=[of])
                            S.op('pool', lambda e: e.tensor_tensor(O.t[:, :T], O.t[:, :T], of.t[:, :T], ALU.add), [O, of], [O])
                            S.op('act', lambda e: e.activation(sq.t[:, :T], O.t[:, :T], AF.Square), [O], [sq])
                            self.mm(pp[0].t[:, :T], pp[0], [(self.cst('ones'), sq.t[:, :T])], [sq, self.CT])
                            S.op('dve', lambda e: e.tensor_scalar(sq.t[:, :T], pp[0].t[:, :T], 1.0 / 128, EPS, ALU.mult, ALU.add), [pp[0]], [sq])
                            S.op('act', lambda e: e.activation(sq.t[:, :T], sq.t[:, :T], AF.Sqrt), [sq], [sq])
                            S.op('dve', lambda e: e.reciprocal(sq.t[:, :T], sq.t[:, :T]), [sq], [sq])
                            S.op('dve', lambda e: e.scalar_tensor_tensor(O.t[:, :T], O.t[:, :T], nw.t[:, 0:1], sq.t[:, :T], ALU.mult, ALU.mult),
                                 [O, nw, sq], [O])
                            wt = self.wload(dr['hgrn_w_in'][j, 32 + hd], 8)
                            self.proj_fm(pp[1], T, wt, h)
                            S.op('act', lambda e: e.activation(sq.t[:, :T], pp[1].t[:, :T], AF.Silu), [pp[1]], [sq])
                            S.op('dve', lambda e, hd=hd: e.tensor_tensor(OG.t[:, hd, :T], O.t[:, :T], sq.t[:, :T], ALU.mult), [O, sq], [OG])
                    if need_out:
                        self.out_proj(Xout, tile, l, dr['hgrn_w_out'][j], 8, OG, 16, 0)


    def phase_ssd(self, l, Xin, Xout, last):
        S = self.S
        dr = self.dr
        with ExitStack() as es:
            self.pn = self.ps(es, 'pn', [128, 512])
            px = self.ps(es, 'px', [128, 512])
            pz = self.ps(es, 'pz', [128, 512])
            self.po = [px, pz]
            pA = self.ps(es, 'pA', [128, 512])
            pS = self.ps(es, 'pS', [128, 512])
            pT = self.ps(es, 'pT', [128, 512])
            pO = self.ps(es, 'pO', [128, 512])
            pU = self.ps(es, 'pU', [128, 512])
            cw = self.sb(es, 'scw', [128, 32, 4])
            cb = self.sb(es, 'scb', [128, 32])
            dtb = self.sb(es, 'dtb', [128, 64])
            nA = self.sb(es, 'nA', [128, 64])
            dsk = self.sb(es, 'dsk', [128, 16])
            nwv = self.sb(es, 'nwv', [128, 16])
            for t_, nm in ((cw, 'ssd_cw'), (cb, 'ssd_cb'), (dtb, 'ssd_dtb'), (nA, 'ssd_alog'), (dsk, 'ssd_dsk'), (nwv, 'ssd_nw')):
                S.dma('sp', t_.t[:], dr[nm], writes=[t_])
            S.op('act', lambda e: e.activation(nA.t[:], nA.t[:], AF.Exp), [nA], [nA])
            S.op('dve', lambda e: e.tensor_scalar(nA.t[:], nA.t[:], -1.0, None, ALU.mult), [nA], [nA])
            CV = [self.sb(es, 'cv%d' % i, [128, 256]) for i in range(32)]
            XS, Bm, Cm = CV[0:16], CV[16:24], CV[24:32]
            YZ = self.sb(es, 'YZ', [128, 16, 256])
            St = [[self.sb(es, 'Ss%d_%d' % (hh, i), [128, 64]) for i in range(2)] for hh in range(32)]
            xb = self.sb(es, 's_xb', [128, 264])
            dtt = self.sb(es, 'dtt', [128, 64])
            dat = self.sb(es, 'dat', [128, 32])
            ew = self.sb(es, 'ew', [128, 64])
            dtw = self.sb(es, 'dtw', [128, 32])
            smk = self.sb(es, 'smk', [128, 128])
            Btok = self.sb(es, 'Btok', [128, 128])
            xdt = self.sb(es, 'xdt', [128, 2, 64])
            xdw = self.sb(es, 'xdw', [128, 2, 64])
            Xh = self.sb(es, 'Xh', [128, 128])
            Wh = self.sb(es, 'Wh', [128, 128])
            Cea = self.sb(es, 'Cea', [128, 128])
            Ob = self.sb(es, 'Ob', [128, 128])
            of = self.sb(es, 'ofs', [128, 128])
            zz = self.sb(es, 'zz', [128, 128])
            lat = self.tiles(256)
            for d in range(2):
                cur = [0] * 32
                for hh in range(32):
                    S.op('pool', lambda e, hh=hh: e.memset(St[hh][0].t[:], 0.0), [], [St[hh][0]])
                order = lat if d == 0 else [lat[0]] + lat[:0:-1]
                M1 = self.cst('m1f') if d == 0 else self.cst('m1b')
                M2 = self.cst('m2f') if d == 0 else self.cst('m2b')
                ones = self.cst('ones')
                ident = self.cst('ident')
                for tile in order:
                    T, t0 = tile['T'], tile['t0']
                    nb = T // 128
                    W = self.load_norm(Xin, tile, 2, 1, l, 1)
                    need_out = (d == 1) and not (tile['ctx'] and last)
                    h = self.h
                    for ch in range(32):
                        wt = self.wload(dr['ssd_w_in'][16 + ch], 8)
                        self.proj_fm(px, W, wt, h)
                        S.op('act', lambda e: e.activation(xb.t[:, :W], px.t[:, :W], AF.Identity), [px], [xb])
                        dst = CV[ch]
                        for k in range(4):
                            if k == 0:
                                S.op('dve', lambda e, ch=ch, dst=dst: e.tensor_scalar(
                                    dst.t[:, :T], xb.t[:, 0:T], cw.t[:, ch, 0:1], cb.t[:, ch:ch + 1], ALU.mult, ALU.add), [xb, cw, cb], [dst])
                            else:
                                S.op('dve', lambda e, ch=ch, dst=dst, k=k: e.scalar_tensor_tensor(
                                    dst.t[:, :T], xb.t[:, k:k + T], cw.t[:, ch, k:k + 1], dst.t[:, :T], ALU.mult, ALU.add), [xb, cw, dst], [dst])
                        S.op('act', lambda e, dst=dst: e.activation(dst.t[:, :T], dst.t[:, :T], AF.Silu), [dst], [dst])
                    for blk in (range(nb) if d == 0 else range(nb - 1, -1, -1)):
                        c0 = blk * 128
                        cs = slice(c0, c0 + 128)
                        wdt = self.wload(dr['ssd_w_in'][48], 8)
                        self.mm(pA.t[:, 0:64], pA, [(h.t[:, kc, 2 + c0:2 + c0 + 128], wdt.t[:, kc, 0:64]) for kc in range(8)], [h, wdt])
                        S.op('dve', lambda e: e.tensor_tensor(dtt.t[:], pA.t[:, 0:64], dtb.t[:], ALU.add), [pA, dtb], [dtt])
                        S.op('act', lambda e: e.activation(dtt.t[:], dtt.t[:], AF.Exp), [dtt], [dtt])
                        S.op('dve', lambda e: e.tensor_scalar(dtt.t[:], dtt.t[:], 1.0, None, ALU.add), [dtt], [dtt])
                        S.op('act', lambda e: e.activation(dtt.t[:], dtt.t[:], AF.Ln), [dtt], [dtt])
                        S.op('dve', lambda e: e.tensor_tensor(dat.t[:], dtt.t[:, d * 32:d * 32 + 32], nA.t[:, d * 32:d * 32 + 32], ALU.mult),
                             [dtt, nA], [dat])
                        self.mm(pA.t[:, 64:96], pA, [(M2, dat.t[:])], [dat, self.CT])
                        self.mm(pA.t[:, 96:128], pA, [(ones, dat.t[:])], [dat, self.CT])
                        S.op('act', lambda e: e.activation(ew.t[:], pA.t[:, 64:128], AF.Exp), [pA], [ew])
                        S.op('dve', lambda e: e.tensor_tensor(dtw.t[:], dtt.t[:, d * 32:d * 32 + 32], ew.t[:, 0:32], ALU.mult), [dtt, ew], [dtw])
                        for g in range(8):
                            self.mm(pS.t[:, 0:128], pS, [(Bm[g].t[:, cs], Cm[g].t[:, cs])], [Bm[g], Cm[g]])
                            S.op('dve', lambda e: e.tensor_tensor(smk.t[:], pS.t[:, 0:128], M1, ALU.mult), [pS, self.CT], [smk])
                            self.mm(pT.t[:, 0:128], pT, [(Bm[g].t[:, cs], ident)], [Bm[g], self.CT])
                            S.op('act', lambda e: e.activation(Btok.t[:], pT.t[:, 0:128], AF.Identity), [pT], [Btok])
                            for hp2 in range(2):
                                hp = 2 * g + hp2
                                self.mm(pT.t[:, 128:256], pT, [(XS[hp].t[:, cs], ident)], [XS[hp], self.CT])
                                for e_ in range(2):
                                    hh = 2 * hp + e_
                                    src = pT.t[:, 128 + 64 * e_:128 + 64 * e_ + 64]
                                    S.op('act', lambda e, src=src, e_=e_, hh=hh: e.activation(
                                        xdt.t[:, e_, :], src, AF.Identity, scale=dtt.t[:, d * 32 + hh:d * 32 + hh + 1]), [pT, dtt], [xdt])
                                    S.op('act', lambda e, src=src, e_=e_, hh=hh: e.activation(
                                        xdw.t[:, e_, :], src, AF.Identity, scale=dtw.t[:, hh:hh + 1]), [pT, dtw], [xdw])
                                    S.op('dve', lambda e, hh=hh: e.tensor_scalar(Xh.t[:], M1, dat.t[:, hh:hh + 1], None, ALU.mult),
                                         [dat, self.CT], [Xh])
                                    self.mm(pS.t[:, 128:256], pS, [(M2, Xh.t[:])], [Xh, self.CT])
                                    self.mm(pS.t[:, 256:384], pS, [(ones, Xh.t[:])], [Xh, self.CT])
                                    S.op('act', lambda e: e.activation(Wh.t[:], pS.t[:, 128:256], AF.Exp), [pS], [Wh])
                                    S.op('act', lambda e: e.activation(Cea.t[:], pS.t[:, 256:384], AF.Exp), [pS], [Cea])
                                    S.op('dve', lambda e: e.tensor_tensor(Wh.t[:], Wh.t[:], smk.t[:], ALU.mult), [Wh, smk], [Wh])
                                    S.op('pool', lambda e, g=g: e.tensor_tensor(Cea.t[:], Cea.t[:], Cm[g].t[:, cs], ALU.mult), [Cea, Cm[g]], [Cea])
                                    Sc = St[hh][cur[hh]]
                                    Sn = St[hh][1 - cur[hh]]
                                    cur[hh] = 1 - cur[hh]
                                    self.mm(pO.t[64 * e_:64 * e_ + 64, 0:128], pO, [(xdt.t[:, e_, :], Wh.t[:]), (Sc.t[:], Cea.t[:])],
                                            [xdt, Wh, Sc, Cea])
                                    self.mm(pU.t[:, 0:64], pU, [(Btok.t[:], xdw.t[:, e_, :])], [Btok, xdw])
                                    S.op('dve', lambda e, Sc=Sc, Sn=Sn, hh=hh: e.scalar_tensor_tensor(
                                        Sn.t[:], Sc.t[:], ew.t[:, 32 + hh:33 + hh], pU.t[:, 0:64], ALU.mult, ALU.add), [Sc, ew, pU], [Sn])
                                S.op('act', lambda e: e.activation(Ob.t[:], pO.t[:, 0:128], AF.Identity), [pO], [Ob])
                                if d == 0:
                                    S.dma('sp', dr['SF'][hp, :, t0 + c0:t0 + c0 + 128], Ob.t[:], reads=[Ob],
                                          writes=self.dbufs('SF%d' % hp, t0 + c0, t0 + c0 + 128))
                                elif need_out:
                                    S.dma('sp', of.t[:], dr['SF'][hp, :, t0 + c0:t0 + c0 + 128],
                                          reads=self.dbufs('SF%d' % hp, t0 + c0, t0 + c0 + 128), writes=[of])
                                    S.op('pool', lambda e: e.tensor_tensor(Ob.t[:], Ob.t[:], of.t[:], ALU.add), [Ob, of], [Ob])
                                    S.op('dve', lambda e, hp=hp: e.scalar_tensor_tensor(
                                        Ob.t[:], XS[hp].t[:, cs], dsk.t[:, hp:hp + 1], Ob.t[:], ALU.mult, ALU.add), [XS[hp], dsk, Ob], [Ob])
                                    wtz = self.wload(dr['ssd_w_in'][hp], 8)
                                    self.mm(pz.t[:, 0:128], pz, [(wtz.t[:, kc, :], h.t[:, kc, 2 + c0:2 + c0 + 128]) for kc in range(8)], [wtz, h])
                                    S.op('act', lambda e: e.activation(zz.t[:], pz.t[:, 0:128], AF.Silu), [pz], [zz])
                                    S.op('dve', lambda e, hp=hp: e.tensor_tensor(YZ.t[:, hp, cs], Ob.t[:], zz.t[:], ALU.mult), [Ob, zz], [YZ])
                    if need_out:
                        sq, rstd = self.sq, self.rstd
                        for hp in range(16):
                            s_ = sq[hp % 2]
                            S.op('act', lambda e, s_=s_, hp=hp: e.activation(s_.t[:, :T], YZ.t[:, hp, :T], AF.Square), [YZ], [s_])
                            S.op('pe', lambda e, s_=s_, hp=hp: e.matmul(self.pn.t[:, :T], ones, s_.t[:, :T], start=(hp == 0), stop=(hp == 15)),
                                 [s_, self.CT], [self.pn])
                        S.op('dve', lambda e: e.tensor_scalar(rstd.t[:, :T], self.pn.t[:, :T], 1.0 / 2048, EPS, ALU.mult, ALU.add), [self.pn], [rstd])
                        S.op('act', lambda e: e.activation(rstd.t[:, :T], rstd.t[:, :T], AF.Sqrt), [rstd], [rstd])
                        S.op('dve', lambda e: e.reciprocal(rstd.t[:, :T], rstd.t[:, :T]), [rstd], [rstd])
                        for hp in range(16):
                            S.op('dve', lambda e, hp=hp: e.scalar_tensor_tensor(
                                YZ.t[:, hp, :T], YZ.t[:, hp, :T], nwv.t[:, hp:hp + 1], rstd.t[:, :T], ALU.mult, ALU.mult), [YZ, nwv, rstd], [YZ])
                        self.out_proj(Xout, tile, l, dr['ssd_w_out'], 16, YZ, 16, 2)

    def phase_final(self, Xin):
        S = self.S
        dr = self.dr
        with ExitStack() as es:
            self.pn = self.ps(es, 'pn', [128, 1024])
            fw = self.sb(es, 'fw', [128, 8])
            S.dma('sp', fw.t[:], dr['fnw'], writes=[fw])
            xt, rstd, sq, pn, xo = self.xt, self.rstd, self.sq, self.pn, self.xo
            for tile in self.tiles():
                if tile['ctx']:
                    continue
                t0, T = tile['t0'], tile['T']
                src = dr[Xin].rearrange("c p t -> p c t")[:, :, t0:t0 + T]
                S.dma('sp', xt.t[:, :, :T], src, reads=self.dbufs(Xin, t0, t0 + T), writes=[xt])
                for c in range(8):
                    s = sq[c % 2]
                    S.op('act', lambda e, s=s, c=c: e.activation(s.t[:, :T], xt.t[:, c, :T], AF.Square), [xt], [s])
                    S.op('pe', lambda e, s=s, c=c: e.matmul(pn.t[:, :T], self.cst('ones'), s.t[:, :T],
                                                            start=(c == 0), stop=(c == 7)), [s, self.CT], [pn])
                S.op('dve', lambda e: e.tensor_scalar(rstd.t[:, :T], pn.t[:, :T], 1.0 / D, EPS, ALU.mult, ALU.add), [pn], [rstd])
                S.op('act', lambda e: e.activation(rstd.t[:, :T], rstd.t[:, :T], AF.Sqrt), [rstd], [rstd])
                S.op('dve', lambda e: e.reciprocal(rstd.t[:, :T], rstd.t[:, :T]), [rstd], [rstd])
                for c in range(8):
                    S.op('dve', lambda e, c=c: e.scalar_tensor_tensor(
                        xo.t[:, c, :T], xt.t[:, c, :T], fw.t[:, c:c + 1], rstd.t[:, :T], ALU.mult, ALU.mult),
                        [xt, rstd, fw], [xo])
                dst = dr['out'].rearrange("c p t -> p c t")[:, :, t0 - CTX:t0 - CTX + T]
                S.dma('sp', dst, xo.t[:, :, :T], reads=[xo], writes=self.dbufs('out', t0, t0 + T))


def prep_inputs(inp, b, kinds, cst_np):
    depth = len(kinds)
    f = lambda a: np.ascontiguousarray(np.asarray(a, dtype=np.float32))
    m = {}
    xall = np.concatenate([inp['ctx'][b], inp['x'][b]], axis=0)
    m['xin'] = f(xall.T.reshape(8, 128, -1))
    cv = np.stack([inp['c'][b], inp['c_ctx']], axis=-1)
    m['cvec'] = f(cv.reshape(8, 128, 2).transpose(1, 0, 2))
    m['consts'] = cst_np
    return m


def prep_shared(inp, kinds):
    depth = len(kinds)
    f = lambda a: np.ascontiguousarray(np.asarray(a, dtype=np.float32))
    m = {}
    m['mod_w'] = np.stack([chunk_w(inp['mod_w'][l]) for l in range(depth)])
    m['mod_b'] = f(fm_vec(inp['mod_b'][:depth]))
    m['w_up'] = np.stack([chunk_w(inp['ffn_w_up'][l]) for l in range(depth)])
    cw = inp['ffn_conv_w'][:depth].reshape(depth, 9, 44, 128)
    m['cw'] = f(cw.transpose(3, 0, 2, 1))
    m['cb'] = f(fm_vec(inp['ffn_conv_b'][:depth]))
    m['w_down'] = np.stack([chunk_w(inp['ffn_w_down'][l]) for l in range(depth)])
    n_lru = max(1, sum(1 for k in kinds if k == 0))
    m['lru_w_in'] = np.stack([chunk_w(inp['lru_w_in'][j]) for j in range(n_lru)])
    m['lru_cw'] = f(inp['lru_conv_w'][:n_lru].reshape(n_lru, 4, 10, 128).transpose(3, 0, 2, 1))
    m['lru_cb'] = f(fm_vec(inp['lru_conv_b'][:n_lru]))
    m['lru_gw'] = f(inp['lru_gate_w'][:n_lru].transpose(0, 1, 4, 2, 3, 5))
    m['lru_gb'] = f(inp['lru_gate_b'][:n_lru].transpose(4, 0, 1, 2, 3))
    m['lru_lam'] = f(fm_vec(inp['lru_lambda'][:n_lru]))
    m['lru_w_out'] = np.stack([chunk_w(inp['lru_w_out'][j]) for j in range(n_lru)])
    m['fnw'] = f(fm_vec(inp['final_norm_w']))
    wi = np.zeros((1024, 6272), np.float32)
    wi[:, :6208] = inp['ssd_w_in'][0]
    m['ssd_w_in'] = chunk_w(wi)
    m['ssd_cw'] = f(inp['ssd_conv_w'][0].reshape(4, 32, 128).transpose(2, 1, 0))
    m['ssd_cb'] = f(fm_vec(inp['ssd_conv_b'][0]))
    m['ssd_dtb'] = f(np.broadcast_to(inp['ssd_dt_bias'][0].reshape(1, 64), (128, 64)))
    m['ssd_alog'] = f(np.broadcast_to(inp['ssd_a_log'][0].reshape(1, 64), (128, 64)))
    m['ssd_dsk'] = f(fm_vec(np.repeat(inp['ssd_d'][0], 64)))
    m['ssd_nw'] = f(fm_vec(inp['ssd_norm_w'][0]))
    m['ssd_w_out'] = chunk_w(inp['ssd_w_out'][0])
    m['hgrn_w_in'] = np.stack([chunk_w(inp['hgrn_w_in'][0])])
    m['hgrn_lg'] = f(fm_vec(inp['hgrn_lb_logits']))
    m['hgrn_nw'] = f(inp['hgrn_norm_w'][:1].T)
    m['hgrn_w_out'] = np.stack([chunk_w(inp['hgrn_w_out'][0])])
    return m


_CACHE = {}


def run(inputs, kinds=(0, 1, 2, 0), ncores=8, trace=False, debug=False):
    inp = {k: np.asarray(v) for k, v in inputs.items()}
    B, L, _ = inp['x'].shape
    key = (L, tuple(kinds))
    if key not in _CACHE:
        p = Prog(L, tuple(kinds), len(kinds))
        p.debug = debug
        p.build()
        _CACHE[key] = p
    p = _CACHE[key]
    shared = prep_shared(inp, kinds)
    in_maps = []
    for c in range(ncores):
        b = c % B
        m = dict(shared)
        m.update(prep_inputs(inp, b, kinds, p.cst_np))
        in_maps.append({k: m[k] for k in p.inputs})
    res = run_bass_kernel_spmd(p.nc, in_maps, core_ids=list(range(ncores)), trace=trace)
    out = np.empty((B, L, D), np.float32)
    for b in range(B):
        y = res.results[b]['yout']
        out[b] = y.reshape(D, L).T
    return out, res


def kernel(**inputs):
    out, _ = run(inputs)
    return out
```

```python
import numpy as np
import concourse.bass as bass
import concourse.mybir as mybir
from concourse.bass_utils import run_bass_kernel_spmd
from contextlib import ExitStack

F32 = mybir.dt.float32
BF16 = mybir.dt.bfloat16
MMDT = BF16
ALU = mybir.AluOpType
AF = mybir.ActivationFunctionType
AX = mybir.AxisListType
ENG = ('pe', 'dve', 'act', 'pool', 'sp')
D = 1024
CTX = 256
EPS = 1e-6


class Buf:
    __slots__ = ('w', 'r')

    def __init__(self):
        self.w = None
        self.r = {}


class Tl:
    def __init__(self, t):
        self.t = t
        self.b = Buf()


class _Cap:
    def __getattr__(self, name):
        def f(*a, **k):
            return (name, a, k)
        return f


_CAP = _Cap()


class Sched:
    def __init__(self, nc, es, n_dma_sems=16):
        self.nc = nc
        self.prog = {e: [] for e in ENG}
        self.semobj = {}
        for e in ENG:
            self.semobj[e] = es.enter_context(nc.semaphore('s_' + e))
        self.cnt = {e: 0 for e in ENG}
        self.seen = {e: {} for e in ENG}
        self.nd = n_dma_sems
        for i in range(n_dma_sems):
            self.semobj[('d', i)] = es.enter_context(nc.semaphore('d%d' % i))
        self.dcnt = [0] * n_dma_sems
        self.dnext = 0
        self.self_sync = {'dve', 'act', 'pool'}

    def _wait(self, E, tok, pend=None):
        key, val = tok
        if key == E and E not in self.self_sync:
            return
        if self.seen[E].get(key, 0) >= val:
            return
        self.seen[E][key] = val
        if pend is not None:
            pend[key] = max(pend.get(key, 0), val)
            return
        sem = self.semobj[key]
        self.prog[E].append(lambda eng, sem=sem, val=val: eng.wait_ge(sem, val))

    def _deps(self, E, reads, writes, pend=None):
        for b in reads:
            if b.w is not None:
                self._wait(E, b.w, pend)
        for b in writes:
            if b.w is not None:
                self._wait(E, b.w, pend)
            for tok in b.r.values():
                self._wait(E, tok, pend)

    def op(self, E, fn, reads=(), writes=()):
        reads = [x.b if isinstance(x, Tl) else x for x in reads]
        writes = [x.b if isinstance(x, Tl) else x for x in writes]
        pend = {}
        self._deps(E, reads, writes, pend)
        pl = list(pend.items())
        for key, val in pl[:-1]:
            sem = self.semobj[key]
            self.prog[E].append(lambda eng, sem=sem, val=val: eng.wait_ge(sem, val))
        att = (self.semobj[pl[-1][0]], pl[-1][1]) if pl else None
        self.cnt[E] += 1
        n = self.cnt[E]
        sem = self.semobj[E]
        rec = fn(_CAP)

        def emit(eng, rec=rec, sem=sem, att=att):
            ins = getattr(eng, rec[0])(*rec[1], **rec[2])
            if att is not None:
                ins._wait_ge(att[0], att[1])
            ins.then_inc(sem, 1)
        self.prog[E].append(emit)
        tok = (E, n)
        for b in reads:
            b.r[E] = tok
        for b in writes:
            b.w = tok
            b.r = {}

    def dma(self, Q, out_ap, in_ap, reads=(), writes=()):
        reads = [x.b if isinstance(x, Tl) else x for x in reads]
        writes = [x.b if isinstance(x, Tl) else x for x in writes]
        i = self.dnext
        self.dnext = (i + 1) % self.nd
        key = ('d', i)
        if self.dcnt[i] > 0:
            self._wait(Q, (key, self.dcnt[i]))
        self._deps(Q, reads, writes)
        self.dcnt[i] += 16
        val = self.dcnt[i]
        sem = self.semobj[key]
        self.prog[Q].append(
            lambda eng, o=out_ap, a=in_ap, sem=sem: eng.dma_start(out=o, in_=a).then_inc(sem, 16))
        tok = (key, val)
        for b in reads:
            b.r[key] = tok
        for b in writes:
            b.w = tok
            b.r = {}

    def barrier(self):
        for E in ENG:
            for P in ENG:
                if P != E and self.cnt[P] > 0:
                    self._wait(E, (P, self.cnt[P]))
            for i in range(self.nd):
                if self.dcnt[i] > 0:
                    self._wait(E, (('d', i), self.dcnt[i]))

    def emit(self):
        with self.nc.Block() as block:
            @block.sync
            def _(e):
                for f in self.prog['sp']:
                    f(e)

            @block.tensor
            def _(e):
                for f in self.prog['pe']:
                    f(e)

            @block.vector
            def _(e):
                for f in self.prog['dve']:
                    f(e)

            @block.scalar
            def _(e):
                for f in self.prog['act']:
                    f(e)

            @block.gpsimd
            def _(e):
                for f in self.prog['pool']:
                    f(e)


def make_consts():
    c = {}
    idx = np.arange(128)
    c['ident'] = np.eye(128, dtype=np.float32)
    c['ones'] = np.ones((128, 128), np.float32)
    c['m1f'] = (idx[:, None] <= idx[None, :]).astype(np.float32)
    c['m1b'] = (idx[:, None] >= idx[None, :]).astype(np.float32)
    c['m2f'] = (idx[:, None] > idx[None, :]).astype(np.float32)
    c['m2b'] = (idx[:, None] < idx[None, :]).astype(np.float32)
    same = (idx[:, None] // 32) == (idx[None, :] // 32)
    c['bdf'] = (same & (idx[:, None] <= idx[None, :])).astype(np.float32)
    c['bdb'] = (same & (idx[:, None] >= idx[None, :])).astype(np.float32)
    rm = np.zeros((128, 128), np.float32)
    for k in range(4):
        rm[32 * k:32 * k + 32, k] = 1.0
    c['rowmask'] = rm
    m01 = np.ones((128, 512), np.float32)
    m01[:, ::32] = 0.0
    names = ['ident', 'ones', 'm1f', 'm1b', 'm2f', 'm2b', 'bdf', 'bdb', 'rowmask']
    arr = np.concatenate([c[n] for n in names] + [m01], axis=1)
    offs = {n: i * 128 for i, n in enumerate(names)}
    offs['m01'] = len(names) * 128
    return np.ascontiguousarray(arr), offs


def chunk_w(w, kc=None):
    K, N = w.shape
    assert K % 128 == 0 and N % 128 == 0
    return np.ascontiguousarray(w.reshape(K // 128, 128, N // 128, 128).transpose(2, 1, 0, 3))


def fm_vec(v):
    sh = v.shape
    n = sh[-1] // 128
    a = v.reshape(sh[:-1] + (n, 128))
    a = np.moveaxis(a, -1, 0)
    return np.ascontiguousarray(a)


class Prog:
    def __init__(self, L, kinds, depth_total):
        self.L = L
        self.LT = CTX + L
        self.kinds = kinds
        self.depth = len(kinds)
        self.nc = bass.Bass("TRN2", target_bir_lowering=False)
        self.inputs = {}
        self.coffs = None
        self.debug = False
        self.dbg_out = {}

    def din(self, name, shape):
        t = self.nc.dram_tensor(name, list(shape), F32, kind="ExternalInput").ap()
        self.inputs[name] = t
        return t

    def dscr(self, name, shape):
        return self.nc.dram_tensor(name, list(shape), F32, kind="Internal").ap()

    def sb(self, es, name, shape, dt=None):
        self.uid = getattr(self, 'uid', 0) + 1
        return Tl(es.enter_context(self.nc.sbuf_tensor('%s_%d' % (name, self.uid), list(shape), dt or F32)))

    def ps(self, es, name, shape):
        self.uid = getattr(self, 'uid', 0) + 1
        return Tl(es.enter_context(self.nc.psum_tensor('%s_%d' % (name, self.uid), list(shape), F32)))

    def dbufs(self, name, lo, hi):
        d = self.dram_b.setdefault(name, {})
        out = []
        for g in range(lo // 256, (hi - 1) // 256 + 1):
            if g not in d:
                d[g] = Buf()
            out.append(d[g])
        return out

    def tiles(self, T=512):
        tl = [dict(t0=0, T=CTX, s0=0, s1=CTX, ctx=True)]
        for i in range(self.L // T):
            tl.append(dict(t0=CTX + i * T, T=T, s0=CTX, s1=self.LT, ctx=False))
        return tl

    def wload(self, ap_chunk, KC):
        S = self.S
        t = self.wring[self.wnext]
        self.wnext = (self.wnext + 1) % len(self.wring)
        S.dma('sp' if MMDT == F32 else 'pool', t.t[:, :KC, :], ap_chunk, writes=[t])
        return t

    def mm(self, out_ap, OUT, pairs, reads):
        n = len(pairs)
        for i, (l, r) in enumerate(pairs):
            self.S.op('pe', lambda e, l=l, r=r, i=i: e.matmul(out_ap, l, r, start=(i == 0), stop=(i == n - 1)),
                      reads, [OUT])

    def proj_fm(self, pt, W, wt, h, c0=0):
        for a in range(0, W, 512):
            b = min(W, a + 512)
            self.mm(pt.t[:, a:b], pt, [(wt.t[:, kc, :], h.t[:, kc, c0 + a:c0 + b]) for kc in range(8)], [wt, h])

    def load_norm(self, Xin, tile, hl, hr, l, which):
        S = self.S
        xt, h, sq, rstd, pn = self.xt, self.h, self.sq, self.rstd, self.pn
        t0, T = tile['t0'], tile['T']
        a, b = t0 - hl, t0 + T + hr
        W = b - a
        lo, hi = max(a, tile['s0']), min(b, tile['s1'])
        src = self.dr[Xin].rearrange("c p t -> p c t")[:, :, lo:hi]
        S.dma('sp', xt.t[:, :, lo - a:hi - a], src, reads=self.dbufs(Xin, lo, hi), writes=[xt])
        if lo > a:
            S.op('pool', lambda e: e.memset(xt.t[:, :, 0:lo - a], 1.0), [], [xt])
        if hi < b:
            S.op('pool', lambda e: e.memset(xt.t[:, :, hi - a:W], 1.0), [], [xt])
        col = 1 if tile['ctx'] else 0
        base = 0 if which == 1 else 24
        for c in range(8):
            s = sq[c % 2]
            S.op('act', lambda e, s=s, c=c: e.activation(s.t[:, :W], xt.t[:, c, :W], AF.Square), [xt], [s])
            for a2 in range(0, W, 512):
                b2 = min(W, a2 + 512)
                S.op('pe', lambda e, s=s, c=c, a2=a2, b2=b2: e.matmul(pn.t[:, a2:b2], self.cst('ones'), s.t[:, a2:b2],
                                                                    start=(c == 0), stop=(c == 7)), [s, self.CT], [pn])
        S.op('dve', lambda e: e.tensor_scalar(rstd.t[:, :W], pn.t[:, :W], 1.0 / D, EPS, ALU.mult, ALU.add), [pn], [rstd])
        S.op('act', lambda e: e.activation(rstd.t[:, :W], rstd.t[:, :W], AF.Sqrt), [rstd], [rstd])
        S.op('dve', lambda e: e.reciprocal(rstd.t[:, :W], rstd.t[:, :W]), [rstd], [rstd])
        for c in range(8):
            tmp = sq[c % 2]
            S.op('dve', lambda e, c=c, tmp=tmp: e.tensor_tensor(tmp.t[:, :W], xt.t[:, c, :W], rstd.t[:, :W], ALU.mult), [xt, rstd], [tmp])
            S.op('act', lambda e, c=c, tmp=tmp: e.activation(h.t[:, c, :W], tmp.t[:, :W], AF.Identity,
                                                              bias=self.modv.t[:, l, base + c, col:col + 1],
                                                              scale=self.modv.t[:, l, base + 8 + c, col:col + 1]), [tmp, self.modv], [h])
        if lo > a:
            S.op('pool', lambda e: e.memset(h.t[:, :, 0:lo - a], 0.0), [], [h])
        if hi < b:
            S.op('pool', lambda e: e.memset(h.t[:, :, hi - a:W], 0.0), [], [h])
        return W

    def dbg(self, name, tl, ap, shape):
        if not self.debug or name in self.dbg_out:
            return
        d = self.nc.dram_tensor('dbg_' + name, list(shape), F32, kind="ExternalOutput").ap()
        self.dbg_out[name] = d
        self.S.dma('sp', d, ap, reads=[tl], writes=[Buf()])

    def cst(self, name, n=128):
        o = self.coffs[name]
        return self.CT.t[:, o:o + n]

    def out_proj(self, Xout, tile, l, wdram, KC, G, gbase, hl, do_store=True):
        S = self.S
        T, t0 = tile['T'], tile['t0']
        col = 1 if tile['ctx'] else 0
        for oc in range(8):
            wt = self.wload(wdram[oc], KC)
            po = self.po[oc % 2]
            self.mm(po.t[:, :T], po, [(wt.t[:, k, :], G.t[:, k, :T]) for k in range(KC)], [wt, G])
            S.op('dve', lambda e, oc=oc, po=po: e.scalar_tensor_tensor(
                self.xo.t[:, oc, :T], po.t[:, :T], self.modv.t[:, l, gbase + oc, col:col + 1],
                self.xt.t[:, oc, hl:hl + T], ALU.mult, ALU.add), [po, self.xt, self.modv], [self.xo])
        dst = self.dr[Xout].rearrange("c p t -> p c t")[:, :, t0:t0 + T]
        S.dma('sp', dst, self.xo.t[:, :, :T], reads=[self.xo], writes=self.dbufs(Xout, t0, t0 + T))

    def build(self):
        nc = self.nc
        L, LT, depth = self.L, self.LT, self.depth
        kinds = self.kinds
        n_lru = max(1, sum(1 for k in kinds if k == 0))
        cst_np, self.coffs = make_consts()
        self.cst_np = cst_np
        dr = self.dr = {}
        dr['X0'] = self.din('xin', [8, 128, LT])
        dr['cvec'] = self.din('cvec', [128, 8, 2])
        dr['consts'] = self.din('consts', list(cst_np.shape))
        dr['mod_w'] = self.din('mod_w', [depth, 48, 128, 8, 128])
        dr['mod_b'] = self.din('mod_b', [128, depth, 48])
        dr['w_up'] = self.din('w_up', [depth, 44, 128, 8, 128])
        dr['cw'] = self.din('cw', [128, depth, 44, 9])
        dr['cb'] = self.din('cb', [128, depth, 44])
        dr['w_down'] = self.din('w_down', [depth, 8, 128, 22, 128])
        dr['lru_w_in'] = self.din('lru_w_in', [n_lru, 20, 128, 8, 128])
        dr['lru_cw'] = self.din('lru_cw', [128, n_lru, 10, 4])
        dr['lru_cb'] = self.din('lru_cb', [128, n_lru, 10])
        dr['lru_gw'] = self.din('lru_gw', [n_lru, 2, 128, 2, 10, 128])
        dr['lru_gb'] = self.din('lru_gb', [128, n_lru, 2, 2, 10])
        dr['lru_lam'] = self.din('lru_lam', [128, n_lru, 2, 10])
        dr['lru_w_out'] = self.din('lru_w_out', [n_lru, 8, 128, 10, 128])
        dr['fnw'] = self.din('fnw', [128, 8])
        dr['ssd_w_in'] = self.din('ssd_w_in', [49, 128, 8, 128])
        dr['ssd_cw'] = self.din('ssd_cw', [128, 32, 4])
        dr['ssd_cb'] = self.din('ssd_cb', [128, 32])
        dr['ssd_dtb'] = self.din('ssd_dtb', [128, 64])
        dr['ssd_alog'] = self.din('ssd_alog', [128, 64])
        dr['ssd_dsk'] = self.din('ssd_dsk', [128, 16])
        dr['ssd_nw'] = self.din('ssd_nw', [128, 16])
        dr['ssd_w_out'] = self.din('ssd_w_out', [8, 128, 16, 128])
        dr['hgrn_w_in'] = self.din('hgrn_w_in', [1, 40, 128, 8, 128])
        dr['hgrn_lg'] = self.din('hgrn_lg', [128, 4, 2, 8])
        dr['hgrn_nw'] = self.din('hgrn_nw', [128, 1])
        dr['hgrn_w_out'] = self.din('hgrn_w_out', [1, 8, 128, 8, 128])
        dr['XA'] = self.dscr('XA', [8, 128, LT])
        dr['XB'] = self.dscr('XB', [8, 128, LT])
        dr['SF'] = self.dscr('SF', [16, 128, LT])
        dr['out'] = nc.dram_tensor('yout', [8, 128, L], F32, kind="ExternalOutput").ap()
        self.dram_b = {}

        es = ExitStack()
        with es:
            S = self.S = Sched(nc, es)
            self.CT = self.sb(es, 'consts_s', list(cst_np.shape))
            S.dma('sp', self.CT.t[:], dr['consts'], writes=[self.CT])
            self.modv = self.sb(es, 'modv', [128, depth, 48, 2])
            self.wring = [self.sb(es, 'wr%d' % i, [128, 22, 128], MMDT) for i in range(4 if MMDT != F32 else 3)]
            self.wnext = 0
            self.xt = self.sb(es, 'xt', [128, 8, 640])
            self.h = self.sb(es, 'h', [128, 8, 640], MMDT)
            self.sq = [self.sb(es, 'sq%d' % i, [128, 640]) for i in range(2)]
            self.rstd = self.sb(es, 'rstd', [128, 640])
            self.xo = self.sb(es, 'xo', [128, 8, 512])
            self.phase_mod(es)
            xin = 'X0'
            nl = 0
            for l, kind in enumerate(kinds):
                last = (l == depth - 1)
                if kind == 0:
                    self.phase_lru(l, nl, xin, 'XB', last)
                    nl += 1
                elif kind == 1:
                    self.phase_hgrn(l, 0, xin, 'XB', last)
                else:
                    self.phase_ssd(l, xin, 'XB', last)
                S.barrier()
                self.phase_ffn(l, 'XB', 'XA', last)
                S.barrier()
                xin = 'XA'
            self.phase_final('XA')
            S.barrier()
            S.emit()
        return nc

    def phase_mod(self, es_outer):
        S = self.S
        dr = self.dr
        with ExitStack() as es:
            cv = self.sb(es, 'cv', [128, 8, 2])
            mb = self.sb(es, 'mb', [128, self.depth, 48])
            pm = [self.ps(es, 'pm%d' % i, [128, 512]) for i in range(2)]
            mring = [self.sb(es, 'mring%d' % i, [128, 8, 128]) for i in range(2)]
            S.dma('sp', cv.t[:], dr['cvec'], writes=[cv])
            S.dma('sp', mb.t[:], dr['mod_b'], writes=[mb])
            S.op('act', lambda e: e.activation(cv.t[:], cv.t[:], AF.Silu), [cv], [cv])
            for l in range(self.depth):
                for j in range(48):
                    wt = mring[j % 2]
                    S.dma('sp', wt.t[:], dr['mod_w'][l, j], writes=[wt])
                    p = pm[j % 2]
                    self.mm(p.t[:, 0:2], p, [(wt.t[:, kc, :], cv.t[:, kc, :]) for kc in range(8)], [wt, cv])
                    S.op('act', lambda e, p=p, l=l, j=j: e.activation(self.modv.t[:, l, j, :], p.t[:, 0:2], AF.Identity,
                                                                      bias=mb.t[:, l, j:j + 1]), [p, mb], [self.modv])
                for base in (8, 32):
                    S.op('dve', lambda e, l=l, base=base: e.tensor_scalar(
                        self.modv.t[:, l, base:base + 8, :], self.modv.t[:, l, base:base + 8, :], 1.0, None, ALU.add),
                        [self.modv], [self.modv])
            S.barrier()

    def phase_ffn(self, l, Xin, Xout, last):
        S = self.S
        dr = self.dr
        with ExitStack() as es:
            self.pn = self.ps(es, 'pn', [128, 1024])
            pu = [self.ps(es, 'pu%d' % i, [128, 1024]) for i in range(2)]
            self.po = [self.ps(es, 'po%d' % i, [128, 512]) for i in range(2)]
            cw = self.sb(es, 'cw', [128, 44, 9])
            cb = self.sb(es, 'cb', [128, 44])
            S.dma('sp', cw.t[:], dr['cw'][:, l], writes=[cw])
            S.dma('sp', cb.t[:], dr['cb'][:, l], writes=[cb])
            U = [self.sb(es, 'U%d' % i, [128, 10, 66]) for i in range(2)]
            Uc = [self.sb(es, 'Uc%d' % i, [128, 264]) for i in range(2)]
            acc = [self.sb(es, 'acc%d' % i, [128, 512]) for i in range(2)]
            G = self.sb(es, 'G', [128, 22, 512], MMDT)
            for u in U:
                S.op('pool', lambda e, u=u: e.memset(u.t[:], 0.0), [], [u])
            for tile in self.tiles():
                if tile['ctx'] and last:
                    continue
                T = tile['T']
                hl = 1 if tile['ctx'] else 64
                W = self.load_norm(Xin, tile, hl, hl, l, 2)
                for j in range(22):
                    for half, jj in ((0, j), (1, 22 + j)):
                        eng = 'dve'
                        wt = self.wload(dr['w_up'][l, jj], 8)
                        p = pu[half]
                        self.proj_fm(p, W, wt, self.h)
                        ac = acc[half]
                        if tile['ctx']:
                            u = Uc[half]
                            S.op('act', lambda e, u=u, p=p: e.activation(u.t[:, 0:W], p.t[:, 0:W], AF.Identity), [p], [u])
                            taps = [(3 + dc, u.t[:, dc:dc + T]) for dc in range(3)]
                            accv = ac.t[:, :T]
                        else:
                            u = U[half]
                            S.op('act', lambda e, u=u, p=p: e.activation(
                                u.t[:, :, 1:65], p.t[:, 0:640].rearrange("p (r c) -> p r c", c=64), AF.Identity), [p], [u])
                            taps = [(dr_ * 3 + dc, u.t[:, dr_:dr_ + 8, dc:dc + 64]) for dr_ in range(3) for dc in range(3)]
                            accv = ac.t[:, :T].rearrange("p (r c) -> p r c", c=64)
                        for ti, (tap, src) in enumerate(taps):
                            if ti == 0:
                                S.op(eng, lambda e, src=src, tap=tap, jj=jj, accv=accv: e.tensor_scalar(
                                    accv, src, cw.t[:, jj, tap:tap + 1], cb.t[:, jj:jj + 1], ALU.mult, ALU.add),
                                    [u, cw, cb], [ac])
                            else:
                                S.op(eng, lambda e, src=src, tap=tap, jj=jj, accv=accv: e.scalar_tensor_tensor(
                                    accv, src, cw.t[:, jj, tap:tap + 1], accv, ALU.mult, ALU.add), [u, cw, ac], [ac])
                    S.op('act', lambda e: e.activation(acc[0].t[:, :T], acc[0].t[:, :T], AF.Silu), [acc[0]], [acc[0]])
                    S.op('dve', lambda e, j=j: e.tensor_tensor(G.t[:, j, :T], acc[0].t[:, :T], acc[1].t[:, :T], ALU.mult),
                         [acc[0], acc[1]], [G])
                dump = (tile['t0'] == CTX and l == 0)
                if dump:
                    self.dbg('h2', self.h, self.h.t[:, :, :W], [128, 8, W])
                    self.dbg('G', G, G.t[:], [128, 22, 512])
                self.out_proj(Xout, tile, l, dr['w_down'][l], 22, G, 40, hl)
                if dump:
                    self.dbg('xo2', self.xo, self.xo.t[:], [128, 8, 512])

    def phase_lru(self, l, j, Xin, Xout, last):
        S = self.S
        dr = self.dr
        with ExitStack() as es:
            self.pn = self.ps(es, 'pn', [128, 1024])
            px = self.ps(es, 'px', [128, 1024])
            pg = [self.ps(es, 'pg%d' % i, [128, 512]) for i in range(2)]
            self.po = [self.ps(es, 'po%d' % i, [128, 512]) for i in range(2)]
            cw = self.sb(es, 'lcw', [128, 10, 4])
            cb = self.sb(es, 'lcb', [128, 10])
            gb = self.sb(es, 'lgb', [128, 2, 2, 10])
            cl = self.sb(es, 'lcl', [128, 2, 10])
            gw = self.sb(es, 'lgw', [128, 2, 10, 128])
            hst = self.sb(es, 'hst', [128, 10])
            S.dma('sp', cw.t[:], dr['lru_cw'][:, j], writes=[cw])
            S.dma('sp', cb.t[:], dr['lru_cb'][:, j], writes=[cb])
            S.dma('sp', gb.t[:], dr['lru_gb'][:, j], writes=[gb])
            S.dma('sp', cl.t[:], dr['lru_lam'][:, j], writes=[cl])
            S.op('act', lambda e: e.activation(cl.t[:], cl.t[:], AF.Exp, scale=-1.0), [cl], [cl])
            S.op('dve', lambda e: e.tensor_scalar(cl.t[:], cl.t[:], 1.0, None, ALU.add), [cl], [cl])
            S.op('act', lambda e: e.activation(cl.t[:], cl.t[:], AF.Ln), [cl], [cl])
            S.op('dve', lambda e: e.tensor_scalar(cl.t[:], cl.t[:], -8.0, None, ALU.mult), [cl], [cl])
            names = ['xb', 'xc', 'rr', 'ii', 'aa', 't1', 'bb', 'hs', 'hf', 'gy']
            tl = {n: self.sb(es, 'l_' + n, [128, 520]) for n in names}
            GR = self.sb(es, 'GR', [128, 10, 512], MMDT)
            lat = self.tiles()
            for d in range(2):
                S.dma('sp', gw.t[:], dr['lru_gw'][j, d], writes=[gw])
                S.op('pool', lambda e: e.memset(hst.t[:], 0.0), [], [hst])
                order = lat if d == 0 else [lat[0]] + lat[:0:-1]
                for tile in order:
                    T, t0 = tile['T'], tile['t0']
                    W = self.load_norm(Xin, tile, 2, 1, l, 1)
                    need_out = (d == 1) and not (tile['ctx'] and last)
                    dump = (tile['t0'] == CTX and l == 0)
                    if dump and d == 0:
                        self.dbg('modv', self.modv, self.modv.t[:, 0, :, :], [128, 48, 2])
                        self.dbg('h1', self.h, self.h.t[:, :, :W], [128, 8, W])
                    for blk in range(10):
                        xb, xc, rr, ii, aa, t1, bb, hs, hf, gy = [tl[n] for n in names]
                        wt = self.wload(dr['lru_w_in'][j, 10 + blk], 8)
                        self.proj_fm(px, W, wt, self.h)
                        S.op('act', lambda e: e.activation(xb.t[:, :W], px.t[:, :W], AF.Identity), [px], [xb])
                        for k in range(4):
                            if k == 0:
                                S.op('dve', lambda e, blk=blk: e.tensor_scalar(
                                    xc.t[:, :T], xb.t[:, 0:T], cw.t[:, blk, 0:1], cb.t[:, blk:blk + 1], ALU.mult, ALU.add),
                                    [xb, cw, cb], [xc])
                            else:
                                S.op('dve', lambda e, blk=blk, k=k: e.scalar_tensor_tensor(
                                    xc.t[:, :T], xb.t[:, k:k + T], cw.t[:, blk, k:k + 1], xc.t[:, :T], ALU.mult, ALU.add),
                                    [xb, cw, xc], [xc])
                        for g, dst in ((0, rr), (1, ii)):
                            p = pg[g]
                            self.mm(p.t[:, :T], p, [(gw.t[:, g, blk, :], xc.t[:, :T])], [gw, xc])
                            S.op('act', lambda e, p=p, dst=dst, g=g, blk=blk: e.activation(
                                dst.t[:, :T], p.t[:, :T], AF.Sigmoid, bias=gb.t[:, d, g, blk:blk + 1]), [p, gb], [dst])
                        S.op('act', lambda e, blk=blk: e.activation(aa.t[:, :T], rr.t[:, :T], AF.Exp,
                                                                     scale=cl.t[:, d, blk:blk + 1]), [rr, cl], [aa])
                        S.op('pool', lambda e: e.tensor_tensor(t1.t[:, :T], aa.t[:, :T], aa.t[:, :T], ALU.mult), [aa], [t1])
                        S.op('pool', lambda e: e.tensor_scalar(t1.t[:, :T], t1.t[:, :T], -1.0, 1.0, ALU.mult, ALU.add), [t1], [t1])
                        S.op('act', lambda e: e.activation(t1.t[:, :T], t1.t[:, :T], AF.Sqrt), [t1], [t1])
                        S.op('pool', lambda e: e.tensor_tensor(bb.t[:, :T], ii.t[:, :T], xc.t[:, :T], ALU.mult), [ii, xc], [bb])
                        S.op('dve', lambda e: e.tensor_tensor(bb.t[:, :T], bb.t[:, :T], t1.t[:, :T], ALU.mult), [bb, t1], [bb])
                        if dump and blk == 0:
                            for nm, tt in (('xb', xb), ('xc', xc), ('rr', rr), ('ii', ii), ('aa', aa), ('bb', bb)):
                                self.dbg('%s%d' % (nm, d), tt, tt.t[:, :512], [128, 512])
                        if d == 0:
                            S.op('dve', lambda e, blk=blk: e.tensor_tensor_scan(
                                hs.t[:, :T], aa.t[:, :T], bb.t[:, :T], hst.t[:, blk:blk + 1], ALU.mult, ALU.add),
                                [aa, bb, hst], [hs])
                            S.op('act', lambda e, blk=blk: e.activation(hst.t[:, blk:blk + 1], hs.t[:, T - 1:T], AF.Identity),
                                 [hs], [hst])
                            if dump and blk == 0:
                                self.dbg('hs0', hs, hs.t[:, :512], [128, 512])
                            dst = dr['SF'][blk, :, t0:t0 + T]
                            S.dma('sp', dst, hs.t[:, :T], reads=[hs], writes=self.dbufs('SF%d' % blk, t0, t0 + T))
                        else:
                            S.op('dve', lambda e, blk=blk: e.tensor_tensor_scan(
                                hs.t[:, T - 1::-1] if False else hs.t[:, 0:T][:, ::-1], aa.t[:, 0:T][:, ::-1], bb.t[:, 0:T][:, ::-1],
                                hst.t[:, blk:blk + 1], ALU.mult, ALU.add), [aa, bb, hst], [hs])
                            S.op('act', lambda e, blk=blk: e.activation(hst.t[:, blk:blk + 1], hs.t[:, 0:1], AF.Identity),
                                 [hs], [hst])
                            if dump and blk == 0:
                                self.dbg('hs1', hs, hs.t[:, :512], [128, 512])
                            if need_out:
                                S.dma('sp', hf.t[:, :T], dr['SF'][blk, :, t0:t0 + T],
                                      reads=self.dbufs('SF%d' % blk, t0, t0 + T), writes=[hf])
                                S.op('pool', lambda e: e.tensor_tensor(hs.t[:, :T], hs.t[:, :T], hf.t[:, :T], ALU.add), [hs, hf], [hs])
                                wt2 = self.wload(dr['lru_w_in'][j, blk], 8)
                                p = pg[0]
                                self.mm(p.t[:, :T], p, [(wt2.t[:, kc, :], self.h.t[:, kc, 2:2 + T]) for kc in range(8)], [wt2, self.h])
                                S.op('act', lambda e, p=p: e.activation(gy.t[:, :T], p.t[:, :T], AF.Gelu_apprx_tanh), [p], [gy])
                                S.op('dve', lambda e, blk=blk: e.tensor_tensor(GR.t[:, blk, :T], gy.t[:, :T], hs.t[:, :T], ALU.mult),
                                     [gy, hs], [GR])
                    if need_out:
                        if dump:
                            self.dbg('GR', GR, GR.t[:], [128, 10, 512])
                        self.out_proj(Xout, tile, l, dr['lru_w_out'][j], 10, GR, 16, 2)
                        if dump:
                            self.dbg('xo1', self.xo, self.xo.t[:], [128, 8, 512])
                    elif d == 1:
                        pass


    def phase_hgrn(self, l, j, Xin, Xout, last):
        S = self.S
        dr = self.dr
        with ExitStack() as es:
            self.pn = self.ps(es, 'pn', [128, 512])
            pp = [self.ps(es, 'pp%d' % i, [128, 512]) for i in range(2)]
            self.po = pp
            pv = self.ps(es, 'pv', [128, 512])
            pS = self.ps(es, 'pS', [128, 512])
            pT = self.ps(es, 'pT', [128, 512])
            pO = self.ps(es, 'pO', [128, 512])
            pU = self.ps(es, 'pU', [128, 512])
            lg = self.sb(es, 'lg', [128, 4, 2, 8])
            lb = self.sb(es, 'lb', [128, 2, 8])
            om = self.sb(es, 'om', [128, 2, 8])
            den = self.sb(es, 'den', [128, 2, 8])
            nw = self.sb(es, 'nw', [128, 1])
            S.dma('sp', lg.t[:], dr['hgrn_lg'], writes=[lg])
            S.dma('sp', nw.t[:], dr['hgrn_nw'][:, j:j + 1], writes=[nw])
            S.op('act', lambda e: e.activation(lg.t[:], lg.t[:], AF.Exp), [lg], [lg])
            S.op('dve', lambda e: e.tensor_tensor(den.t[:], lg.t[:, 0], lg.t[:, 1], ALU.add), [lg], [den])
            for k in (2, 3):
                S.op('dve', lambda e, k=k: e.tensor_tensor(den.t[:], den.t[:], lg.t[:, k], ALU.add), [lg, den], [den])
            S.op('dve', lambda e: e.reciprocal(den.t[:], den.t[:]), [den], [den])
            S.op('dve', lambda e: e.memset(lb.t[:], 0.0), [], [lb])
            for k in range(1, l + 1):
                S.op('dve', lambda e, k=k: e.tensor_tensor(lb.t[:], lb.t[:], lg.t[:, k], ALU.add), [lg, lb], [lb])
            S.op('dve', lambda e: e.tensor_tensor(lb.t[:], lb.t[:], den.t[:], ALU.mult), [lb, den], [lb])
            S.op('dve', lambda e: e.tensor_scalar(om.t[:], lb.t[:], -1.0, 1.0, ALU.mult, ALU.add), [lb], [om])
            names = ['q', 'fg', 'lf', 'kk', 'gc', 'G', 'ep', 'em', 'qd', 'ki', 'ke', 'O', 'of', 'sq']
            tl = {n: self.sb(es, 'g_' + n, [128, 512]) for n in names}
            VT = self.sb(es, 'VT', [128, 4, 5, 128])
            sm = self.sb(es, 'sm', [128, 128])
            keTs = self.sb(es, 'keTs', [128, 128])
            St = [[self.sb(es, 'St%d_%d' % (hd, i), [128, 128]) for i in range(2)] for hd in range(8)]
            OG = self.sb(es, 'OG', [128, 8, 512], MMDT)
            lat = self.tiles()
            for d in range(2):
                cur = [0] * 8
                for hd in range(8):
                    S.op('pool', lambda e, hd=hd: e.memset(St[hd][0].t[:], 0.0), [], [St[hd][0]])
                order = lat if d == 0 else [lat[0]] + lat[:0:-1]
                bd = self.cst('bdf') if d == 0 else self.cst('bdb')
                for tile in order:
                    T, t0 = tile['T'], tile['t0']
                    nb = T // 128
                    ncnk = T // 32
                    self.load_norm(Xin, tile, 0, 0, l, 1)
                    need_out = (d == 1) and not (tile['ctx'] and last)
                    h = self.h
                    for hd in range(8):
                        q, fg, lf, kk, gc, G, ep, em, qd, ki, ke, O, of, sq = [tl[n] for n in names]
                        wt = self.wload(dr['hgrn_w_in'][j, hd], 8)
                        self.proj_fm(pp[0], T, wt, h)
                        S.op('act', lambda e: e.activation(q.t[:, :T], pp[0].t[:, :T], AF.Silu), [pp[0]], [q])
                        wt = self.wload(dr['hgrn_w_in'][j, 16 + 8 * d + hd], 8)
                        self.proj_fm(pp[1], T, wt, h)
                        S.op('act', lambda e: e.activation(fg.t[:, :T], pp[1].t[:, :T], AF.Sigmoid), [pp[1]], [fg])
                        S.op('dve', lambda e: e.tensor_scalar(fg.t[:, :T], fg.t[:, :T], om.t[:, d, hd:hd + 1], lb.t[:, d, hd:hd + 1],
                                                              ALU.mult, ALU.add), [fg, om, lb], [fg])
                        S.op('act', lambda e: e.activation(lf.t[:, :T], fg.t[:, :T], AF.Ln), [fg], [lf])
                        S.op('pool', lambda e: e.tensor_scalar(kk.t[:, :T], fg.t[:, :T], -1.0, 1.0, ALU.mult, ALU.add), [fg], [kk])
                        S.op('dve', lambda e: e.tensor_tensor_scan(gc.t[:, :T], self.cst('m01', 512)[:, :T], lf.t[:, :T], 0.0,
                                                                   ALU.mult, ALU.add), [lf, self.CT], [gc])
                        v3 = lambda t_: t_.t[:, :T].rearrange("p (c j) -> p c j", j=32)
                        if d == 0:
                            G = gc
                        else:
                            S.op('dve', lambda e: e.tensor_tensor(v3(G), v3(gc), v3(gc)[:, :, 31:32].to_broadcast([128, ncnk, 32]),
                                                                  ALU.subtract), [gc], [G])
                            S.op('dve', lambda e: e.tensor_tensor(G.t[:, :T], lf.t[:, :T], G.t[:, :T], ALU.subtract), [lf, G], [G])
                        S.op('act', lambda e: e.activation(ep.t[:, :T], G.t[:, :T], AF.Exp), [G], [ep])
                        S.op('act', lambda e: e.activation(em.t[:, :T], G.t[:, :T], AF.Exp, scale=-1.0), [G], [em])
                        S.op('dve', lambda e: e.tensor_tensor(qd.t[:, :T], q.t[:, :T], ep.t[:, :T], ALU.mult), [q, ep], [qd])
                        S.op('pool', lambda e: e.tensor_tensor(ki.t[:, :T], kk.t[:, :T], em.t[:, :T], ALU.mult), [kk, em], [ki])
                        eo = 31 if d == 0 else 0
                        S.op('dve', lambda e: e.tensor_tensor(v3(ke), v3(ki), v3(ep)[:, :, eo:eo + 1].to_broadcast([128, ncnk, 32]),
                                                              ALU.mult), [ki, ep], [ke])
                        wtv = self.wload(dr['hgrn_w_in'][j, 8 + hd], 8)
                        for blk in range(nb):
                            self.mm(pv.t[:, :128], pv, [(h.t[:, kc, blk * 128:(blk + 1) * 128], wtv.t[:, kc, :]) for kc in range(8)],
                                    [h, wtv])
                            S.op('act', lambda e, blk=blk: e.activation(VT.t[:, blk, 0, :], pv.t[:, :128], AF.Identity), [pv], [VT])
                            for c in range(4):
                                S.op('act', lambda e, blk=blk, c=c: e.activation(
                                    VT.t[:, blk, 1 + c, :], pv.t[:, :128], AF.Identity,
                                    scale=self.cst('rowmask')[:, c:c + 1]), [pv, self.CT], [VT])
                        for blk in (range(nb) if d == 0 else range(nb - 1, -1, -1)):
                            c0 = blk * 128
                            self.mm(pS.t[:, :128], pS, [(ki.t[:, c0:c0 + 128], qd.t[:, c0:c0 + 128])], [ki, qd])
                            S.op('dve', lambda e: e.tensor_tensor(sm.t[:], pS.t[:, :128], bd, ALU.mult), [pS, self.CT], [sm])
                            self.mm(pT.t[:, :128], pT, [(ke.t[:, c0:c0 + 128], self.cst('ident'))], [ke, self.CT])
                            S.op('act', lambda e: e.activation(keTs.t[:], pT.t[:, :128], AF.Identity), [pT], [keTs])
                            for c in (range(4) if d == 0 else range(3, -1, -1)):
                                cc = c0 + 32 * c
                                Sc = St[hd][cur[hd]]
                                Sn = St[hd][1 - cur[hd]]
                                cur[hd] = 1 - cur[hd]
                                self.mm(pO.t[:, 32 * c:32 * c + 32], pO,
                                        [(VT.t[:, blk, 0, :], sm.t[:, 32 * c:32 * c + 32]), (Sc.t[:], qd.t[:, cc:cc + 32])],
                                        [VT, sm, Sc, qd])
                                self.mm(pU.t[:, :128], pU, [(keTs.t[:], VT.t[:, blk, 1 + c, :])], [keTs, VT])
                                S.op('dve', lambda e, Sc=Sc, Sn=Sn, cc=cc: e.scalar_tensor_tensor(
                                    Sn.t[:], Sc.t[:], ep.t[:, cc + eo:cc + eo + 1], pU.t[:, :128], ALU.mult, ALU.add),
                                    [Sc, ep, pU], [Sn])
                            S.op('act', lambda e, c0=c0: e.activation(O.t[:, c0:c0 + 128], pO.t[:, :128], AF.Identity), [pO], [O])
                        if d == 0:
                            S.dma('sp', dr['SF'][hd, :, t0:t0 + T], O.t[:, :T], reads=[O], writes=self.dbufs('SF%d' % hd, t0, t0 + T))
                        elif need_out:
                            S.dma('sp', of.t[:, :T], dr['SF'][hd, :, t0:t0 + T], reads=self.dbufs('SF%d' % hd, t0, t0 + T), writes=[of])
                            S.op('pool', lambda e: e.tensor_tensor(O.t[:, :T], O.t[:, :T], of.t[:, :T], ALU.add), [O, of], [O])
                            S.op('act', lambda e: e.activation(sq.t[:, :T], O.t[:, :T], AF.Square), [O], [sq])
                            self.mm(pp[0].t[:, :T], pp[0], [(self.cst('ones'), sq.t[:, :T])], [sq, self.CT])
                            S.op('dve', lambda e: e.tensor_scalar(sq.t[:, :T], pp[0].t[:, :T], 1.0 / 128, EPS, ALU.mult, ALU.add), [pp[0]], [sq])
                            S.op('act', lambda e: e.activation(sq.t[:, :T], sq.t[:, :T], AF.Sqrt), [sq], [sq])
                            S.op('dve', lambda e: e.reciprocal(sq.t[:, :T], sq.t[:, :T]), [sq], [sq])
                            S.op('dve', lambda e: e.scalar_tensor_tensor(O.t[:, :T], O.t[:, :T], nw.t[:, 0:1], sq.t[:, :T], ALU.mult, ALU.mult),
                                 [O, nw, sq], [O])
                            wt = self.wload(dr['hgrn_w_in'][j, 32 + hd], 8)
                            self.proj_fm(pp[1], T, wt, h)
                            S.op('act', lambda e: e.activation(sq.t[:, :T], pp[1].t[:, :T], AF.Silu), [pp[1]], [sq])
                            S.op('dve', lambda e, hd=hd: e.tensor_tensor(OG.t[:, hd, :T], O.t[:, :T], sq.t[:, :T], ALU.mult), [O, sq], [OG])
                    if need_out:
                        self.out_proj(Xout, tile, l, dr['hgrn_w_out'][j], 8, OG, 16, 0)


    def phase_ssd(self, l, Xin, Xout, last):
        S = self.S
        dr = self.dr
        with ExitStack() as es:
            self.pn = self.ps(es, 'pn', [128, 512])
            px = self.ps(es, 'px', [128, 512])
            pz = self.ps(es, 'pz', [128, 512])
            self.po = [px, pz]
            pA = self.ps(es, 'pA', [128, 512])
            pS = self.ps(es, 'pS', [128, 512])
            pT = self.ps(es, 'pT', [128, 512])
            pO = self.ps(es, 'pO', [128, 512])
            pU = self.ps(es, 'pU', [128, 512])
            cw = self.sb(es, 'scw', [128, 32, 4])
            cb = self.sb(es, 'scb', [128, 32])
            dtb = self.sb(es, 'dtb', [128, 64])
            nA = self.sb(es, 'nA', [128, 64])
            dsk = self.sb(es, 'dsk', [128, 16])
            nwv = self.sb(es, 'nwv', [128, 16])
            for t_, nm in ((cw, 'ssd_cw'), (cb, 'ssd_cb'), (dtb, 'ssd_dtb'), (nA, 'ssd_alog'), (dsk, 'ssd_dsk'), (nwv, 'ssd_nw')):
                S.dma('sp', t_.t[:], dr[nm], writes=[t_])
            S.op('act', lambda e: e.activation(nA.t[:], nA.t[:], AF.Exp), [nA], [nA])
            S.op('dve', lambda e: e.tensor_scalar(nA.t[:], nA.t[:], -1.0, None, ALU.mult), [nA], [nA])
            CV = [self.sb(es, 'cv%d' % i, [128, 256]) for i in range(32)]
            XS, Bm, Cm = CV[0:16], CV[16:24], CV[24:32]
            YZ = self.sb(es, 'YZ', [128, 16, 256])
            YN = self.sb(es, 'YN', [128, 16, 256], MMDT)
            St = [[self.sb(es, 'Ss%d_%d' % (hh, i), [128, 64]) for i in range(2)] for hh in range(32)]
            xb = self.sb(es, 's_xb', [128, 264])
            dtt = self.sb(es, 'dtt', [128, 64])
            dat = self.sb(es, 'dat', [128, 32])
            ew = self.sb(es, 'ew', [128, 64])
            dtw = self.sb(es, 'dtw', [128, 32])
            smk = self.sb(es, 'smk', [128, 128])
            Btok = self.sb(es, 'Btok', [128, 128])
            xdt = self.sb(es, 'xdt', [128, 2, 64])
            xdw = self.sb(es, 'xdw', [128, 2, 64])
            Xh = self.sb(es, 'Xh', [128, 128])
            Wh = self.sb(es, 'Wh', [128, 128])
            Cea = self.sb(es, 'Cea', [128, 128])
            Ob = self.sb(es, 'Ob', [128, 128])
            of = self.sb(es, 'ofs', [128, 128])
            zz = self.sb(es, 'zz', [128, 128])
            lat = self.tiles(256)
            for d in range(2):
                cur = [0] * 32
                for hh in range(32):
                    S.op('pool', lambda e, hh=hh: e.memset(St[hh][0].t[:], 0.0), [], [St[hh][0]])
                order = lat if d == 0 else [lat[0]] + lat[:0:-1]
                M1 = self.cst('m1f') if d == 0 else self.cst('m1b')
                M2 = self.cst('m2f') if d == 0 else self.cst('m2b')
                ones = self.cst('ones')
                ident = self.cst('ident')
                for tile in order:
                    T, t0 = tile['T'], tile['t0']
                    nb = T // 128
                    W = self.load_norm(Xin, tile, 2, 1, l, 1)
                    need_out = (d == 1) and not (tile['ctx'] and last)
                    h = self.h
                    for ch in range(32):
                        wt = self.wload(dr['ssd_w_in'][16 + ch], 8)
                        self.proj_fm(px, W, wt, h)
                        S.op('act', lambda e: e.activation(xb.t[:, :W], px.t[:, :W], AF.Identity), [px], [xb])
                        dst = CV[ch]
                        for k in range(4):
                            if k == 0:
                                S.op('dve', lambda e, ch=ch, dst=dst: e.tensor_scalar(
                                    dst.t[:, :T], xb.t[:, 0:T], cw.t[:, ch, 0:1], cb.t[:, ch:ch + 1], ALU.mult, ALU.add), [xb, cw, cb], [dst])
                            else:
                                S.op('dve', lambda e, ch=ch, dst=dst, k=k: e.scalar_tensor_tensor(
                                    dst.t[:, :T], xb.t[:, k:k + T], cw.t[:, ch, k:k + 1], dst.t[:, :T], ALU.mult, ALU.add), [xb, cw, dst], [dst])
                        S.op('act', lambda e, dst=dst: e.activation(dst.t[:, :T], dst.t[:, :T], AF.Silu), [dst], [dst])
                    for blk in (range(nb) if d == 0 else range(nb - 1, -1, -1)):
                        c0 = blk * 128
                        cs = slice(c0, c0 + 128)
                        wdt = self.wload(dr['ssd_w_in'][48], 8)
                        self.mm(pA.t[:, 0:64], pA, [(h.t[:, kc, 2 + c0:2 + c0 + 128], wdt.t[:, kc, 0:64]) for kc in range(8)], [h, wdt])
                        S.op('dve', lambda e: e.tensor_tensor(dtt.t[:], pA.t[:, 0:64], dtb.t[:], ALU.add), [pA, dtb], [dtt])
                        S.op('act', lambda e: e.activation(dtt.t[:], dtt.t[:], AF.Exp), [dtt], [dtt])
                        S.op('dve', lambda e: e.tensor_scalar(dtt.t[:], dtt.t[:], 1.0, None, ALU.add), [dtt], [dtt])
                        S.op('act', lambda e: e.activation(dtt.t[:], dtt.t[:], AF.Ln), [dtt], [dtt])
                        S.op('dve', lambda e: e.tensor_tensor(dat.t[:], dtt.t[:, d * 32:d * 32 + 32], nA.t[:, d * 32:d * 32 + 32], ALU.mult),
                             [dtt, nA], [dat])
                        self.mm(pA.t[:, 64:96], pA, [(M2, dat.t[:])], [dat, self.CT])
                        self.mm(pA.t[:, 96:128], pA, [(ones, dat.t[:])], [dat, self.CT])
                        S.op('act', lambda e: e.activation(ew.t[:], pA.t[:, 64:128], AF.Exp), [pA], [ew])
                        S.op('dve', lambda e: e.tensor_tensor(dtw.t[:], dtt.t[:, d * 32:d * 32 + 32], ew.t[:, 0:32], ALU.mult), [dtt, ew], [dtw])
                        for g in range(8):
                            self.mm(pS.t[:, 0:128], pS, [(Bm[g].t[:, cs], Cm[g].t[:, cs])], [Bm[g], Cm[g]])
                            S.op('dve', lambda e: e.tensor_tensor(smk.t[:], pS.t[:, 0:128], M1, ALU.mult), [pS, self.CT], [smk])
                            self.mm(pT.t[:, 0:128], pT, [(Bm[g].t[:, cs], ident)], [Bm[g], self.CT])
                            S.op('act', lambda e: e.activation(Btok.t[:], pT.t[:, 0:128], AF.Identity), [pT], [Btok])
                            for hp2 in range(2):
                                hp = 2 * g + hp2
                                self.mm(pT.t[:, 128:256], pT, [(XS[hp].t[:, cs], ident)], [XS[hp], self.CT])
                                for e_ in range(2):
                                    hh = 2 * hp + e_
                                    src = pT.t[:, 128 + 64 * e_:128 + 64 * e_ + 64]
                                    S.op('act', lambda e, src=src, e_=e_, hh=hh: e.activation(
                                        xdt.t[:, e_, :], src, AF.Identity, scale=dtt.t[:, d * 32 + hh:d * 32 + hh + 1]), [pT, dtt], [xdt])
                                    S.op('act', lambda e, src=src, e_=e_, hh=hh: e.activation(
                                        xdw.t[:, e_, :], src, AF.Identity, scale=dtw.t[:, hh:hh + 1]), [pT, dtw], [xdw])
                                    S.op('dve', lambda e, hh=hh: e.tensor_scalar(Xh.t[:], M1, dat.t[:, hh:hh + 1], None, ALU.mult),
                                         [dat, self.CT], [Xh])
                                    self.mm(pS.t[:, 128:256], pS, [(M2, Xh.t[:])], [Xh, self.CT])
                                    self.mm(pS.t[:, 256:384], pS, [(ones, Xh.t[:])], [Xh, self.CT])
                                    S.op('act', lambda e: e.activation(Wh.t[:], pS.t[:, 128:256], AF.Exp), [pS], [Wh])
                                    S.op('act', lambda e: e.activation(Cea.t[:], pS.t[:, 256:384], AF.Exp), [pS], [Cea])
                                    S.op('dve', lambda e: e.tensor_tensor(Wh.t[:], Wh.t[:], smk.t[:], ALU.mult), [Wh, smk], [Wh])
                                    S.op('pool', lambda e, g=g: e.tensor_tensor(Cea.t[:], Cea.t[:], Cm[g].t[:, cs], ALU.mult), [Cea, Cm[g]], [Cea])
                                    Sc = St[hh][cur[hh]]
                                    Sn = St[hh][1 - cur[hh]]
                                    cur[hh] = 1 - cur[hh]
                                    self.mm(pO.t[64 * e_:64 * e_ + 64, 0:128], pO, [(xdt.t[:, e_, :], Wh.t[:]), (Sc.t[:], Cea.t[:])],
                                            [xdt, Wh, Sc, Cea])
                                    self.mm(pU.t[:, 0:64], pU, [(Btok.t[:], xdw.t[:, e_, :])], [Btok, xdw])
                                    S.op('dve', lambda e, Sc=Sc, Sn=Sn, hh=hh: e.scalar_tensor_tensor(
                                        Sn.t[:], Sc.t[:], ew.t[:, 32 + hh:33 + hh], pU.t[:, 0:64], ALU.mult, ALU.add), [Sc, ew, pU], [Sn])
                                S.op('act', lambda e: e.activation(Ob.t[:], pO.t[:, 0:128], AF.Identity), [pO], [Ob])
                                if d == 0:
                                    S.dma('sp', dr['SF'][hp, :, t0 + c0:t0 + c0 + 128], Ob.t[:], reads=[Ob],
                                          writes=self.dbufs('SF%d' % hp, t0 + c0, t0 + c0 + 128))
                                elif need_out:
                                    S.dma('sp', of.t[:], dr['SF'][hp, :, t0 + c0:t0 + c0 + 128],
                                          reads=self.dbufs('SF%d' % hp, t0 + c0, t0 + c0 + 128), writes=[of])
                                    S.op('pool', lambda e: e.tensor_tensor(Ob.t[:], Ob.t[:], of.t[:], ALU.add), [Ob, of], [Ob])
                                    S.op('dve', lambda e, hp=hp: e.scalar_tensor_tensor(
                                        Ob.t[:], XS[hp].t[:, cs], dsk.t[:, hp:hp + 1], Ob.t[:], ALU.mult, ALU.add), [XS[hp], dsk, Ob], [Ob])
                                    wtz = self.wload(dr['ssd_w_in'][hp], 8)
                                    self.mm(pz.t[:, 0:128], pz, [(wtz.t[:, kc, :], h.t[:, kc, 2 + c0:2 + c0 + 128]) for kc in range(8)], [wtz, h])
                                    S.op('act', lambda e: e.activation(zz.t[:], pz.t[:, 0:128], AF.Silu), [pz], [zz])
                                    S.op('dve', lambda e, hp=hp: e.tensor_tensor(YZ.t[:, hp, cs], Ob.t[:], zz.t[:], ALU.mult), [Ob, zz], [YZ])
                    if need_out:
                        sq, rstd = self.sq, self.rstd
                        for hp in range(16):
                            s_ = sq[hp % 2]
                            S.op('act', lambda e, s_=s_, hp=hp: e.activation(s_.t[:, :T], YZ.t[:, hp, :T], AF.Square), [YZ], [s_])
                            S.op('pe', lambda e, s_=s_, hp=hp: e.matmul(self.pn.t[:, :T], ones, s_.t[:, :T], start=(hp == 0), stop=(hp == 15)),
                                 [s_, self.CT], [self.pn])
                        S.op('dve', lambda e: e.tensor_scalar(rstd.t[:, :T], self.pn.t[:, :T], 1.0 / 2048, EPS, ALU.mult, ALU.add), [self.pn], [rstd])
                        S.op('act', lambda e: e.activation(rstd.t[:, :T], rstd.t[:, :T], AF.Sqrt), [rstd], [rstd])
                        S.op('dve', lambda e: e.reciprocal(rstd.t[:, :T], rstd.t[:, :T]), [rstd], [rstd])
                        for hp in range(16):
                            S.op('dve', lambda e, hp=hp: e.scalar_tensor_tensor(
                                YN.t[:, hp, :T], YZ.t[:, hp, :T], nwv.t[:, hp:hp + 1], rstd.t[:, :T], ALU.mult, ALU.mult), [YZ, nwv, rstd], [YN])
                        self.out_proj(Xout, tile, l, dr['ssd_w_out'], 16, YN, 16, 2)

    def phase_final(self, Xin):
        S = self.S
        dr = self.dr
        with ExitStack() as es:
            self.pn = self.ps(es, 'pn', [128, 1024])
            fw = self.sb(es, 'fw', [128, 8])
            S.dma('sp', fw.t[:], dr['fnw'], writes=[fw])
            xt, rstd, sq, pn, xo = self.xt, self.rstd, self.sq, self.pn, self.xo
            for tile in self.tiles():
                if tile['ctx']:
                    continue
                t0, T = tile['t0'], tile['T']
                src = dr[Xin].rearrange("c p t -> p c t")[:, :, t0:t0 + T]
                S.dma('sp', xt.t[:, :, :T], src, reads=self.dbufs(Xin, t0, t0 + T), writes=[xt])
                for c in range(8):
                    s = sq[c % 2]
                    S.op('act', lambda e, s=s, c=c: e.activation(s.t[:, :T], xt.t[:, c, :T], AF.Square), [xt], [s])
                    S.op('pe', lambda e, s=s, c=c: e.matmul(pn.t[:, :T], self.cst('ones'), s.t[:, :T],
                                                            start=(c == 0), stop=(c == 7)), [s, self.CT], [pn])
                S.op('dve', lambda e: e.tensor_scalar(rstd.t[:, :T], pn.t[:, :T], 1.0 / D, EPS, ALU.mult, ALU.add), [pn], [rstd])
                S.op('act', lambda e: e.activation(rstd.t[:, :T], rstd.t[:, :T], AF.Sqrt), [rstd], [rstd])
                S.op('dve', lambda e: e.reciprocal(rstd.t[:, :T], rstd.t[:, :T]), [rstd], [rstd])
                for c in range(8):
                    S.op('dve', lambda e, c=c: e.scalar_tensor_tensor(
                        xo.t[:, c, :T], xt.t[:, c, :T], fw.t[:, c:c + 1], rstd.t[:, :T], ALU.mult, ALU.mult),
                        [xt, rstd, fw], [xo])
                dst = dr['out'].rearrange("c p t -> p c t")[:, :, t0 - CTX:t0 - CTX + T]
                S.dma('sp', dst, xo.t[:, :, :T], reads=[xo], writes=self.dbufs('out', t0, t0 + T))


def prep_inputs(inp, b, kinds, cst_np):
    depth = len(kinds)
    f = lambda a: np.ascontiguousarray(np.asarray(a, dtype=np.float32))
    m = {}
    xall = np.concatenate([inp['ctx'][b], inp['x'][b]], axis=0)
    m['xin'] = f(xall.T.reshape(8, 128, -1))
    cv = np.stack([inp['c'][b], inp['c_ctx']], axis=-1)
    m['cvec'] = f(cv.reshape(8, 128, 2).transpose(1, 0, 2))
    m['consts'] = cst_np
    return m


def prep_shared(inp, kinds):
    depth = len(kinds)
    f = lambda a: np.ascontiguousarray(np.asarray(a, dtype=np.float32))
    m = {}
    m['mod_w'] = np.stack([chunk_w(inp['mod_w'][l]) for l in range(depth)])
    m['mod_b'] = f(fm_vec(inp['mod_b'][:depth]))
    m['w_up'] = np.stack([chunk_w(inp['ffn_w_up'][l]) for l in range(depth)])
    cw = inp['ffn_conv_w'][:depth].reshape(depth, 9, 44, 128)
    m['cw'] = f(cw.transpose(3, 0, 2, 1))
    m['cb'] = f(fm_vec(inp['ffn_conv_b'][:depth]))
    m['w_down'] = np.stack([chunk_w(inp['ffn_w_down'][l]) for l in range(depth)])
    n_lru = max(1, sum(1 for k in kinds if k == 0))
    m['lru_w_in'] = np.stack([chunk_w(inp['lru_w_in'][j]) for j in range(n_lru)])
    m['lru_cw'] = f(inp['lru_conv_w'][:n_lru].reshape(n_lru, 4, 10, 128).transpose(3, 0, 2, 1))
    m['lru_cb'] = f(fm_vec(inp['lru_conv_b'][:n_lru]))
    m['lru_gw'] = f(inp['lru_gate_w'][:n_lru].transpose(0, 1, 4, 2, 3, 5))
    m['lru_gb'] = f(inp['lru_gate_b'][:n_lru].transpose(4, 0, 1, 2, 3))
    m['lru_lam'] = f(fm_vec(inp['lru_lambda'][:n_lru]))
    m['lru_w_out'] = np.stack([chunk_w(inp['lru_w_out'][j]) for j in range(n_lru)])
    m['fnw'] = f(fm_vec(inp['final_norm_w']))
    wi = np.zeros((1024, 6272), np.float32)
    wi[:, :6208] = inp['ssd_w_in'][0]
    m['ssd_w_in'] = chunk_w(wi)
    m['ssd_cw'] = f(inp['ssd_conv_w'][0].reshape(4, 32, 128).transpose(2, 1, 0))
    m['ssd_cb'] = f(fm_vec(inp['ssd_conv_b'][0]))
    m['ssd_dtb'] = f(np.broadcast_to(inp['ssd_dt_bias'][0].reshape(1, 64), (128, 64)))
    m['ssd_alog'] = f(np.broadcast_to(inp['ssd_a_log'][0].reshape(1, 64), (128, 64)))
    m['ssd_dsk'] = f(fm_vec(np.repeat(inp['ssd_d'][0], 64)))
    m['ssd_nw'] = f(fm_vec(inp['ssd_norm_w'][0]))
    m['ssd_w_out'] = chunk_w(inp['ssd_w_out'][0])
    m['hgrn_w_in'] = np.stack([chunk_w(inp['hgrn_w_in'][0])])
    m['hgrn_lg'] = f(fm_vec(inp['hgrn_lb_logits']))
    m['hgrn_nw'] = f(inp['hgrn_norm_w'][:1].T)
    m['hgrn_w_out'] = np.stack([chunk_w(inp['hgrn_w_out'][0])])
    return m


_CACHE = {}


def run(inputs, kinds=(0, 1, 2, 0), ncores=8, trace=False, debug=False):
    inp = {k: np.asarray(v) for k, v in inputs.items()}
    B, L, _ = inp['x'].shape
    key = (L, tuple(kinds))
    if key not in _CACHE:
        p = Prog(L, tuple(kinds), len(kinds))
        p.debug = debug
        p.build()
        _CACHE[key] = p
    p = _CACHE[key]
    shared = prep_shared(inp, kinds)
    in_maps = []
    for c in range(ncores):
        b = c % B
        m = dict(shared)
        m.update(prep_inputs(inp, b, kinds, p.cst_np))
        in_maps.append({k: m[k] for k in p.inputs})
    res = run_bass_kernel_spmd(p.nc, in_maps, core_ids=list(range(ncores)), trace=trace)
    out = np.empty((B, L, D), np.float32)
    for b in range(B):
        y = res.results[b]['yout']
        out[b] = y.reshape(D, L).T
    return out, res


def kernel(**inputs):
    out, _ = run(inputs)
    return out
```
